# Optimizing a Trainium2 kernel written in Bass

```python
import jax, jax.numpy as jnp
from jax import lax
import numpy as np

D_MODEL = 2048
BATCH = 16
SEQ = 256
DEPTH = 2
DEC_BATCH = 2
DEC_SEQ = 1024
PAST_LEN = 256

GRID_W = 64
HEAD_DIM = 128
N_Q_HEADS = 8
N_KV_HEADS = 2
Q_PER_KV = N_Q_HEADS // N_KV_HEADS
ATT_WIDTH = N_Q_HEADS * HEAD_DIM
KV_WIDTH = N_KV_HEADS * HEAD_DIM
HG_HEADS = 8
HG_DK = 128
HG_DV = 128
HG_KW = HG_HEADS * HG_DK
HG_VW = HG_HEADS * HG_DV
IN_AB = ATT_WIDTH + 2 * KV_WIDTH + 3 * HG_KW + 2 * HG_VW
MIX_WIDTH = ATT_WIDTH + HG_VW
N_AB_LAYERS = (DEPTH + 1) // 2
N_C_LAYERS = DEPTH // 2
POOL_WINDOWS = (2, 4, 8, 16)
POOL_GROUP = D_MODEL // len(POOL_WINDOWS)
D_FF = 4 * D_MODEL
Q_BLOCK = 128
HG_CHUNK = 16
ROPE_THETA = 10000.0
ROPE_HALF = HEAD_DIM // 2
EPS = 1e-6
N_MOD = 6

kernel_name = "hybrid_diffusion_prefix_ctx_step"


def rmsnorm(x, gain):
    xf = x.astype(jnp.float32)
    y = xf * lax.rsqrt(jnp.mean(xf * xf, axis=-1, keepdims=True) + EPS)
    return (y * gain.astype(jnp.float32)).astype(x.dtype)


def adaln_params(cvec, w_ada_l, b_ada_l):
    m = jax.nn.silu(cvec) @ w_ada_l + b_ada_l
    return jnp.split(m[:, None, :], N_MOD, axis=-1)


def modulate(x, gain, shift, scale):
    return rmsnorm(x, gain) * (1 + scale) + shift


def axial_rope(x):
    T = x.shape[1]
    rows = T // GRID_W
    row = jnp.repeat(jnp.arange(rows), GRID_W).astype(jnp.float32)
    col = jnp.tile(jnp.arange(GRID_W), rows).astype(jnp.float32)
    inv = ROPE_THETA ** (-jnp.arange(0, ROPE_HALF, 2, dtype=jnp.float32) / ROPE_HALF)
    ar = row[:, None] * inv
    ac = col[:, None] * inv
    ang = jnp.concatenate([ar, ar, ac, ac], axis=-1)
    bshape = (1, T) + (1,) * (x.ndim - 3) + (HEAD_DIM,)
    cos = jnp.cos(ang).reshape(bshape)
    sin = jnp.sin(ang).reshape(bshape)
    xf = x.astype(jnp.float32)
    a, b, c, d = jnp.split(xf, 4, axis=-1)
    rot = jnp.concatenate([-b, a, -d, c], axis=-1)
    return (xf * cos + rot * sin).astype(x.dtype)


def attend(q, k, v):
    B, Tq = q.shape[0], q.shape[1]
    nb = Tq // Q_BLOCK
    qb = jnp.moveaxis(q.reshape(B, nb, Q_BLOCK, N_KV_HEADS, Q_PER_KV, HEAD_DIM), 1, 0)
    scale = HEAD_DIM ** -0.5

    def one_block(qblk):
        s = jnp.einsum('bqhgd,bkhd->bhgqk', qblk, k).astype(jnp.float32) * scale
        p = jax.nn.softmax(s, axis=-1)
        return jnp.einsum('bhgqk,bkhd->bqhgd', p.astype(v.dtype), v)

    out = lax.map(one_block, qb)
    return jnp.moveaxis(out, 0, 1).reshape(B, Tq, ATT_WIDTH)


def hgrn_scan(q, logf, k, i, s0):
    B, T, H, _ = q.shape
    N = T // HG_CHUNK
    r = lambda a: a.reshape(B, N, HG_CHUNK, H, a.shape[-1])
    q, logf, k, i = r(q), r(logf), r(k), r(i)
    b = jnp.cumsum(logf, axis=2)
    mask = jnp.tril(jnp.ones((HG_CHUNK, HG_CHUNK), dtype=bool))
    diff = b[:, :, :, None] - b[:, :, None]
    decay = jnp.exp(jnp.where(mask[None, None, :, :, None, None], diff, -jnp.inf))
    att = jnp.einsum('bnthd,bnshd,bntshd->bnhts', q, k, decay)
    o = jnp.einsum('bnhts,bnshe->bnthe', att, i)
    b_last = b[:, :, -1]
    dS = jnp.einsum('bnshd,bnshe->bnhde', k * jnp.exp(b_last[:, :, None] - b), i)
    a = jnp.exp(b_last)

    def step(S, xs):
        a_n, dS_n = xs
        return a_n[..., None] * S + dS_n, S

    s_fin, s_starts = lax.scan(step, s0, (jnp.swapaxes(a, 0, 1), jnp.swapaxes(dS, 0, 1)))
    s_starts = jnp.swapaxes(s_starts, 0, 1)
    o = o + jnp.einsum('bnthd,bnhde->bnthe', q * jnp.exp(b), s_starts)
    return o.reshape(B, T, H, i.shape[-1]), s_fin


def mixer_ab(h, w_in, w_out, q_gain, k_gain, o_gain, lb, ctx_k=None, ctx_v=None, s0_f=None, s0_b=None):
    B, T, _ = h.shape
    latent = ctx_k is not None
    sizes = [ATT_WIDTH, KV_WIDTH, KV_WIDTH, HG_KW, HG_KW, HG_KW, HG_VW, HG_VW]
    pts, acc = [], 0
    for s in sizes[:-1]:
        acc += s
        pts.append(acc)
    q, k, v, hq, zf, zb, hi, hg = jnp.split(h @ w_in, pts, axis=-1)
    q = rmsnorm(q.reshape(B, T, N_KV_HEADS, Q_PER_KV, HEAD_DIM), q_gain)
    k = rmsnorm(k.reshape(B, T, N_KV_HEADS, HEAD_DIM), k_gain)
    v = v.reshape(B, T, N_KV_HEADS, HEAD_DIM)
    new_k, new_v = k, v
    if latent:
        q = axial_rope(q)
        k = jnp.concatenate([axial_rope(k), ctx_k.astype(k.dtype)], axis=1)
        v = jnp.concatenate([v, ctx_v.astype(v.dtype)], axis=1)
    att = attend(q, k, v)
    kshape = (B, T, HG_HEADS, HG_DK)
    hq = jax.nn.silu(hq.astype(jnp.float32)).reshape(kshape)
    f_f = lb[0] + (1 - lb[0]) * jax.nn.sigmoid(zf.astype(jnp.float32).reshape(kshape))
    f_b = lb[1] + (1 - lb[1]) * jax.nn.sigmoid(zb.astype(jnp.float32).reshape(kshape))
    iv = hi.astype(jnp.float32).reshape(B, T, HG_HEADS, HG_DV)
    if s0_f is None:
        s0_f = jnp.zeros((B, HG_HEADS, HG_DK, HG_DV), jnp.float32)
        s0_b = jnp.zeros((B, HG_HEADS, HG_DK, HG_DV), jnp.float32)
    o_f, s_f = hgrn_scan(hq, jnp.log(f_f), 1 - f_f, iv, s0_f.astype(jnp.float32))
    fl = lambda a: a[:, ::-1]
    o_b, s_b = hgrn_scan(fl(hq), fl(jnp.log(f_b)), fl(1 - f_b), fl(iv), s0_b.astype(jnp.float32))
    o = o_f + fl(o_b)
    o = rmsnorm(o, o_gain).reshape(B, T, HG_VW) * jax.nn.silu(hg.astype(jnp.float32))
    out = jnp.concatenate([att, o.astype(att.dtype)], axis=-1) @ w_out
    return out, (new_k, new_v, s_f.astype(h.dtype), s_b.astype(h.dtype))


def pool_mixer(h, w_pool_l, scale_l):
    B, T, _ = h.shape
    hf = h.astype(jnp.float32)
    cs = jnp.concatenate([jnp.zeros((B, 1, D_MODEL), jnp.float32), jnp.cumsum(hf, axis=1)], axis=1)
    t = jnp.arange(T)
    outs = []
    for g, w in enumerate(POOL_WINDOWS):
        lo_c, hi_c = g * POOL_GROUP, (g + 1) * POOL_GROUP
        lo = jnp.clip(t - w // 2, 0, T)
        hi = jnp.clip(t + w - w // 2, 0, T)
        csg = cs[:, :, lo_c:hi_c]
        mean = (csg[:, hi] - csg[:, lo]) / (hi - lo).astype(jnp.float32)[None, :, None]
        pooled = (mean - hf[:, :, lo_c:hi_c]).astype(h.dtype)
        outs.append(pooled @ w_pool_l[g])
    return jnp.concatenate(outs, axis=-1) * scale_l


def mlp(h, w1, w2):
    return jnp.square(jax.nn.relu(h @ w1)) @ w2


def setup_inputs(seed: int = 0) -> dict:
    key = jax.random.key(seed)
    ks = jax.random.split(key, 24)
    f32 = jnp.float32
    nrm = lambda k, shape, s=1.0: jax.random.normal(k, shape, f32) * s
    return {
        "x_prompt": nrm(ks[0], (BATCH, SEQ, D_MODEL)),
        "x_sample": nrm(ks[1], (DEC_BATCH, DEC_SEQ, D_MODEL)),
        "cache_k": nrm(ks[2], (DEC_BATCH, N_AB_LAYERS, PAST_LEN, N_KV_HEADS, HEAD_DIM)),
        "cache_v": nrm(ks[3], (DEC_BATCH, N_AB_LAYERS, PAST_LEN, N_KV_HEADS, HEAD_DIM)),
        "state_hgrn_fwd": nrm(ks[4], (DEC_BATCH, N_AB_LAYERS, HG_HEADS, HG_DK, HG_DV), 0.5),
        "state_hgrn_bwd": nrm(ks[5], (DEC_BATCH, N_AB_LAYERS, HG_HEADS, HG_DK, HG_DV), 0.5),
        "c": nrm(ks[6], (DEC_BATCH, D_MODEL)),
        "c_ctx": nrm(ks[7], (D_MODEL,)),
        "w_ada": nrm(ks[8], (DEPTH, D_MODEL, N_MOD * D_MODEL), 0.5 * D_MODEL ** -0.5),
        "b_ada": nrm(ks[9], (DEPTH, N_MOD * D_MODEL), 0.01),
        "norm_mix": 1.0 + nrm(ks[10], (DEPTH, D_MODEL), 0.02),
        "norm_mlp": 1.0 + nrm(ks[11], (DEPTH, D_MODEL), 0.02),
        "w_in_ab": nrm(ks[12], (N_AB_LAYERS, D_MODEL, IN_AB), D_MODEL ** -0.5),
        "w_out_ab": nrm(ks[13], (N_AB_LAYERS, MIX_WIDTH, D_MODEL), MIX_WIDTH ** -0.5),
        "q_norm": 1.0 + nrm(ks[14], (N_AB_LAYERS, HEAD_DIM), 0.02),
        "k_norm": 1.0 + nrm(ks[15], (N_AB_LAYERS, HEAD_DIM), 0.02),
        "hg_norm": 1.0 + nrm(ks[16], (N_AB_LAYERS, HG_DV), 0.02),
        "lb_raw": nrm(ks[17], (2, DEPTH + 1, HG_KW), 0.5),
        "w_pool": nrm(ks[18], (N_C_LAYERS, len(POOL_WINDOWS), POOL_GROUP, POOL_GROUP), POOL_GROUP ** -0.5),
        "pool_scale": 1.0 + nrm(ks[19], (N_C_LAYERS, D_MODEL), 0.1),
        "w_mlp_in": nrm(ks[20], (DEPTH, D_MODEL, D_FF), D_MODEL ** -0.5),
        "w_mlp_out": nrm(ks[21], (DEPTH, D_FF, D_MODEL), D_FF ** -0.5),
        "final_norm": 1.0 + nrm(ks[22], (D_MODEL,), 0.02),
    }


def reference(x_prompt, x_sample, cache_k, cache_v, state_hgrn_fwd, state_hgrn_bwd, c, c_ctx,
              w_ada, b_ada, norm_mix, norm_mlp, w_in_ab, w_out_ab, q_norm, k_norm, hg_norm, lb_raw,
              w_pool, pool_scale, w_mlp_in, w_mlp_out, final_norm):
    lb_all = jnp.cumsum(jax.nn.softmax(lb_raw.astype(jnp.float32), axis=1), axis=1)

    def run_stream(x, cvec, ctx):
        states = []
        for l in range(DEPTH):
            sh1, sc1, g1, sh2, sc2, g2 = adaln_params(cvec, w_ada[l], b_ada[l])
            h = modulate(x, norm_mix[l], sh1, sc1)
            j = l // 2
            if l % 2 == 0:
                lb_l = lb_all[:, l].reshape(2, HG_HEADS, HG_DK)
                if ctx is None:
                    mix, st = mixer_ab(h, w_in_ab[j], w_out_ab[j], q_norm[j], k_norm[j], hg_norm[j], lb_l)
                else:
                    mix, st = mixer_ab(h, w_in_ab[j], w_out_ab[j], q_norm[j], k_norm[j], hg_norm[j], lb_l,
                                       ctx_k=ctx[0][:, j], ctx_v=ctx[1][:, j],
                                       s0_f=ctx[2][:, j], s0_b=ctx[3][:, j])
                states.append(st)
            else:
                mix = pool_mixer(h, w_pool[j], pool_scale[j])
            x = x + g1 * mix
            h = modulate(x, norm_mlp[l], sh2, sc2)
            x = x + g2 * mlp(h, w_mlp_in[l], w_mlp_out[l])
        return rmsnorm(x, final_norm), states

    y_prompt, st_p = run_stream(x_prompt, c_ctx[None, :], None)
    y_sample, _ = run_stream(x_sample, c, (cache_k, cache_v, state_hgrn_fwd, state_hgrn_bwd))

    new_k = jnp.stack([s[0] for s in st_p], axis=1)
    new_v = jnp.stack([s[1] for s in st_p], axis=1)
    new_s_fwd = jnp.stack([s[2] for s in st_p], axis=1)
    new_s_bwd = jnp.stack([s[3] for s in st_p], axis=1)
    return (y_prompt, y_sample, new_k, new_v, new_s_fwd, new_s_bwd)
```

```python
import os
from contextlib import ExitStack

import numpy as np
import concourse.bass as bass
import concourse.mybir as mybir
from concourse.bass_utils import run_bass_kernel_spmd

F32 = mybir.dt.float32
BF16 = mybir.dt.bfloat16
AF = mybir.ActivationFunctionType
ALU = mybir.AluOpType

D = 2048
NCH = 16
NP = 512
SEG = 272
NM = NP + SEG
LS = 1024
EPS = 1e-6
SEG_STARTS = (0, 248, 504, 752)
CH = 64
SAME_ENG_SYNC = True

V_NMIX, V_NMLP, V_FIN, V_PSC, V_BADA, V_QN, V_KN, V_HN, V_LB, V_CV, V_OH, NV = (
    0, 32, 64, 80, 96, 288, 289, 290, 291, 339, 371, 376)

NB_ADA = 24
WB_ADA0 = 0
WB_ADA1 = 24
WB_IN = 48
WB_OUT = 61
WB_MLP0 = 65
WB_POOL = 97
WB_MLP1 = 98
NWB = 130


class Res:
    __slots__ = ("name", "w", "r")

    def __init__(self, name):
        self.name = name
        self.w = None
        self.r = {}


class DSem:
    def __init__(self, name, eng):
        self.name = name
        self.eng = eng
        self.n = 0
        self.h = None
        self.res = Res("sem_" + name)
        self.last = None


class Op:
    __slots__ = ("eng", "fn", "dsem", "dval", "waits", "signal", "sigval", "idx")


class Prog:
    ENGS = ("pe", "act", "dve", "pool", "sp")
    BLK = {"pe": "tensor", "act": "scalar", "dve": "vector", "pool": "gpsimd", "sp": "sync"}

    def __init__(self, nc):
        self.nc = nc
        self.ops = []
        self.dsems = []
        self.last = {e: None for e in self.ENGS}
        self.pending = {e: None for e in self.ENGS}

    def dsem(self, name, eng):
        d = DSem(name, eng)
        self.dsems.append(d)
        return d

    def barrier(self):
        snap = [p for p in self.last.values() if p is not None]
        snap += [d.last for d in self.dsems if d.last is not None]
        for e in self.ENGS:
            self.pending[e] = list(snap) + (self.pending[e] or [])

    def add(self, eng, fn, reads=(), writes=(), dsem=None):
        op = Op()
        op.eng, op.fn, op.dsem = eng, fn, dsem
        op.signal, op.sigval, op.dval = False, 0, 0
        op.idx = len(self.ops)
        deps = {}
        raw = set()

        def dep(p):
            if p is not None:
                deps[p.idx] = p

        for r in reads:
            dep(r.w)
            if r.w is not None:
                raw.add(r.w.idx)
        for w in writes:
            dep(w.w)
            for q in w.r.values():
                dep(q)
        if self.pending[eng]:
            for p in self.pending[eng]:
                dep(p)
            self.pending[eng] = None
        if dsem is not None:
            assert dsem.eng == eng
            for q in dsem.res.r.values():
                dep(q)
        rkey = eng if dsem is None else ("d", dsem.name)
        waits = []
        for p in deps.values():
            if p.dsem is not None:
                ds = p.dsem
                waits.append(("d", ds, ds.n * 16))
                ds.res.r[rkey] = op
            else:
                if p.eng == eng and (eng == "pe" or not SAME_ENG_SYNC or p.idx not in raw):
                    continue
                p.signal = True
                waits.append(("c", p, 0))
        op.waits = waits
        if dsem is not None:
            dsem.n += 1
            op.dval = dsem.n * 16
            dsem.res.r = {}
            dsem.last = op
        for r in reads:
            r.r[rkey] = op
        for w in writes:
            w.w = op
            w.r = {}
        self.ops.append(op)
        self.last[eng] = op
        return op


    def check_deadlock(self):
        per = {e: [o for o in self.ops if o.eng == e] for e in self.ENGS}
        for e in self.ENGS:
            c = 0
            for o in per[e]:
                if o.signal and o.dsem is None:
                    c += 1
                    o.sigval = c
        val = {}
        pos = {e: 0 for e in self.ENGS}
        progress = True
        while progress:
            progress = False
            for e in self.ENGS:
                while pos[e] < len(per[e]):
                    o = per[e][pos[e]]
                    ok = True
                    for kind, obj, v in o.waits:
                        if kind == "d":
                            if val.get(("d", obj.name), 0) < v:
                                ok = False
                        else:
                            if val.get(("c", obj.eng), 0) < obj.sigval:
                                ok = False
                    if not ok:
                        break
                    if o.dsem is not None:
                        val[("d", o.dsem.name)] = val.get(("d", o.dsem.name), 0) + 16
                    elif o.signal:
                        val[("c", e)] = val.get(("c", e), 0) + 1
                    pos[e] += 1
                    progress = True
        stuck = {e: (pos[e], len(per[e])) for e in self.ENGS if pos[e] < len(per[e])}
        return stuck, {e: len(per[e]) for e in self.ENGS}, val

    def emit(self):
        nc = self.nc
        per = {e: [o for o in self.ops if o.eng == e] for e in self.ENGS}
        for e in self.ENGS:
            c = 0
            for o in per[e]:
                if o.signal and o.dsem is None:
                    c += 1
                    o.sigval = c
        with ExitStack() as st:
            esem = {e: st.enter_context(nc.semaphore("es_" + e)) for e in self.ENGS}
            for d in self.dsems:
                d.h = st.enter_context(nc.semaphore("ds_" + d.name))
            block = st.enter_context(nc.Block())

            def run(e, h):
                known = {}
                for o in per[e]:
                    for kind, obj, val in o.waits:
                        if kind == "d":
                            key, sem, v = ("d", obj.name), obj.h, val
                        else:
                            key, sem, v = ("c", obj.eng), esem[obj.eng], obj.sigval
                        if known.get(key, 0) >= v:
                            continue
                        h.wait_ge(sem, v)
                        known[key] = v
                    ins = o.fn(h) if o.fn is not None else None
                    if ins is None:
                        assert not o.signal and o.dsem is None
                        continue
                    if o.dsem is not None:
                        ins.then_inc(o.dsem.h, 16)
                    elif o.signal:
                        ins.then_inc(esem[e], 1)

            for e in self.ENGS:
                if not per[e]:
                    continue
                deco = getattr(block, self.BLK[e])

                def body(h, e=e):
                    run(e, h)

                deco(body)


def _rope_tables():
    t = np.arange(LS)
    row = (t // 64).astype(np.float32)
    col = (t % 64).astype(np.float32)
    inv = (10000.0 ** (-np.arange(0, 64, 2, dtype=np.float32) / 64.0)).astype(np.float32)
    ar = row[:, None] * inv
    ac = col[:, None] * inv
    ang = np.concatenate([ar, ar, ac, ac], axis=-1)
    out = np.zeros((128, 2, LS), np.float32)
    out[:, 0, :] = np.cos(ang).T
    out[:, 1, :] = np.sin(ang).T
    return out


def _pool_matrix(T, lo_t, n, w):
    P = np.zeros((n, n), np.float32)
    for o in range(n):
        t = lo_t + o
        lo = min(max(t - w // 2, 0), T)
        hi = min(max(t + w - w // 2, 0), T)
        cnt = float(hi - lo)
        for s in range(lo, hi):
            i = s - lo_t
            if 0 <= i < n:
                P[i, o] += 1.0 / cnt
        P[o, o] -= 1.0
    return P


def _const_b16():
    c = np.zeros((128, 5, 128), np.float32)
    c[:, 0, :] = np.eye(128)
    c[:, 1, :] = 1.0
    s = np.arange(128)[:, None]
    t = np.arange(128)[None, :]
    same = (s // CH) == (t // CH)
    c[:, 2, :] = -(same & (s <= t)).astype(np.float32)
    c[:, 3, :] = -(same & (s >= t)).astype(np.float32)
    R = np.zeros((128, 128), np.float32)
    for j in range(32):
        R[32 + j, j] = -1.0
        R[j, 32 + j] = 1.0
        R[96 + j, 64 + j] = -1.0
        R[64 + j, 96 + j] = 1.0
    c[:, 4, :] = R
    return c.reshape(128, 640)


def _const_f32():
    c = np.zeros((128, 256 + LS), np.float32)
    c[:, 0:128] = np.eye(128)
    c[:, 128:256] = 1.0
    m = np.ones(LS, np.float32)
    m[::CH] = 0.0
    c[:, 256:] = m[None, :]
    return c


def _ptab(q):
    out = np.zeros((128, 4, 2 * 256 + 3 * SEG), np.float32)
    for g, w in enumerate((2, 4, 8, 16)):
        Pp = _pool_matrix(256, 0, 256, w)
        for ti in range(2):
            out[:, g, ti * 256:(ti + 1) * 256] = Pp[ti * 128:(ti + 1) * 128, :]
        Ps = _pool_matrix(LS, SEG_STARTS[q], SEG, w)
        for ti in range(3):
            rows = Ps[ti * 128:min((ti + 1) * 128, SEG), :]
            out[:rows.shape[0], g, 512 + ti * SEG:512 + (ti + 1) * SEG] = rows
    return out.reshape(128, -1)


def _blockify(W, col0, ncols=512):
    sub = W[:, col0:col0 + ncols]
    return np.ascontiguousarray(sub.reshape(16, 128, ncols).transpose(1, 0, 2)).reshape(128, 16 * ncols)


def _build_weight_blocks(w_ada, w_in_ab, w_out_ab, w_pool, w_mlp_in, w_mlp_out):
    wb = np.empty((NWB, 128, 8192), np.float32)
    for l in range(2):
        for nb in range(NB_ADA):
            wb[WB_ADA0 + l * NB_ADA + nb] = _blockify(w_ada[l], nb * 512)
    Wi = w_in_ab[0]
    wb[WB_IN + 0] = _blockify(Wi, 0)
    wb[WB_IN + 1] = _blockify(Wi, 512)
    wb[WB_IN + 2] = _blockify(Wi, 1024)
    wb[WB_IN + 3] = _blockify(Wi, 5632)
    wb[WB_IN + 4] = _blockify(Wi, 5632 + 512)
    for h in range(8):
        cols = np.concatenate([np.arange(base + h * 128, base + (h + 1) * 128)
                               for base in (1536, 2560, 3584, 4608)])
        sub = Wi[:, cols]
        wb[WB_IN + 5 + h] = _blockify(sub, 0)
    for cb in range(4):
        wb[WB_OUT + cb] = _blockify(w_out_ab[0], cb * 512)
    for l, base in ((0, WB_MLP0), (1, WB_MLP1)):
        for s in range(4):
            for j in range(4):
                wb[base + s * 8 + j] = _blockify(w_mlp_in[l], s * 2048 + j * 512)
            W2s = w_mlp_out[l][s * 2048:(s + 1) * 2048, :]
            for cb in range(4):
                wb[base + s * 8 + 4 + cb] = _blockify(W2s, cb * 512)
    pw = np.zeros((2048, 512), np.float32)
    for g in range(4):
        pw[g * 512:(g + 1) * 512, :] = w_pool[0, g]
    wb[WB_POOL] = _blockify(pw, 0)
    return wb


def _build_vecs(core, c, c_ctx, b_ada, norm_mix, norm_mlp, q_norm, k_norm, hg_norm, lb_raw,
                pool_scale, final_norm):
    b = core // 4
    q = core % 4
    v = np.zeros((128, NV), np.float32)
    fm = lambda x: np.asarray(x, np.float32).reshape(-1, 128).T
    for l in range(2):
        v[:, V_NMIX + l * 16:V_NMIX + (l + 1) * 16] = fm(norm_mix[l])
        v[:, V_NMLP + l * 16:V_NMLP + (l + 1) * 16] = fm(norm_mlp[l])
        v[:, V_BADA + l * 96:V_BADA + (l + 1) * 96] = fm(b_ada[l])
    v[:, V_FIN:V_FIN + 16] = fm(final_norm)
    v[:, V_PSC:V_PSC + 16] = fm(pool_scale[0])
    v[:, V_QN] = q_norm[0]
    v[:, V_KN] = k_norm[0]
    v[:, V_HN] = hg_norm[0]
    for d in range(2):
        for j in range(3):
            v[:, V_LB + d * 24 + j * 8:V_LB + d * 24 + (j + 1) * 8] = fm(lb_raw[d, j])
    cv = np.stack([fm(c_ctx), fm(c[b])], axis=-1)
    v[:, V_CV:V_CV + 32] = cv.reshape(128, 32)
    v[:, V_OH + q] = 1.0
    return v


class Builder:
    def __init__(self, stage=99, dbg=(), wsched=None):
        self.wsched_in = wsched
        self.stage = stage
        self.dbg = dbg
        self.nc = bass.Bass("TRN2", target_bir_lowering=False)
        self.P = Prog(self.nc)
        self.st = ExitStack()
        self.bank_rr = 0
        self.dbg_out = []

    def dram_in(self, name, shape):
        return self.nc.dram_tensor(name, list(shape), F32, kind="ExternalInput").ap()

    def dram_out(self, name, shape):
        return self.nc.dram_tensor(name, list(shape), F32, kind="ExternalOutput").ap()

    def sb(self, name, shape, dt=F32):
        return self.st.enter_context(self.nc.sbuf_tensor("s_" + name, list(shape), dt))

    def mm(self, out, lhsT, rhs, start, stop, r, w):
        return self.P.add("pe", lambda h: h.matmul(out, lhsT, rhs, start=start, stop=stop), r, w)

    def tr(self, out, in_, ident, r, w):
        return self.P.add("pe", lambda h: h.transpose(out, in_, ident), r, w)

    def act(self, out, in_, func, r, w, bias=None, scale=None):
        kw = {}
        if bias is not None:
            kw["bias"] = bias
        if scale is not None:
            kw["scale"] = scale
        return self.P.add("act", lambda h: h.activation(out, in_, func, **kw), r, w)

    def tt(self, out, in0, in1, op, r, w, eng="dve"):
        return self.P.add(eng, lambda h: h.tensor_tensor(out, in0, in1, op), r, w)

    def ts(self, out, in0, s1, s2, op0, op1, r, w, eng="dve"):
        if op1 is None:
            return self.P.add(eng, lambda h: h.tensor_scalar(out, in0, s1, None, op0), r, w)
        return self.P.add(eng, lambda h: h.tensor_scalar(out, in0, s1, s2, op0, op1), r, w)

    def stt(self, out, in0, sc, in1, op0, op1, r, w, eng="dve"):
        return self.P.add(eng, lambda h: h.scalar_tensor_tensor(out, in0, sc, in1, op0, op1), r, w)

    def cp(self, out, in_, r, w, eng="dve"):
        return self.P.add(eng, lambda h: h.tensor_copy(out, in_), r, w)

    def dma(self, eng, out, in_, r, w, dsem):
        return self.P.add(eng, lambda h: h.dma_start(out=out, in_=in_), r, w, dsem=dsem)

    def bank(self):
        i = self.bank_rr
        self.bank_rr = (self.bank_rr + 1) % 6
        return self.pb[i], self.pbr[i]


    def rsqrt(self, bk, c, n, bkr):
        if not hasattr(self, "_cbias"):
            self._cbias = {}
        self.act(self.rstd[:, 0:n], bk[:, 0:n], AF.Ln, [bkr, self.r_mod], [self.r_rstd], bias=self.cconst(c))
        self.act(self.rstd[:, 0:n], self.rstd[:, 0:n], AF.Exp, [self.r_rstd], [self.r_rstd], scale=-0.5)

    def cconst(self, c):
        key = float(c)
        if key not in self._cc:
            idx = len(self._cc)
            ap = self.cct[:, idx:idx + 1]
            self.P.add("dve", lambda h, ap=ap, key=key: h.memset(ap, key), [], [self.r_mod])
            self._cc[key] = ap
        return self._cc[key]

    def wnext(self, blk):
        i = self.wpos
        self.wpos += 1
        if self.wsched is None:
            self.req.append(blk)
            self._wissue(i, blk)
        else:
            assert self.wsched[i] == blk, (i, blk, self.wsched[i])
            while self.wissued < min(len(self.wsched), i + 2):
                self._wissue(self.wissued, self.wsched[self.wissued])
                self.wissued += 1
        s = i % 2
        return self.wring[s], self.wres[s]

    def _wissue(self, i, blk):
        s = i % 2
        dst = self.wring[s]
        self.dma("pool", dst[:, :, :].rearrange("p a b -> p (a b)"), self.wb[blk, :, :], [],
                 [self.wres[s]], self.wsem[s])

    def build(self):
        nc, P = self.nc, self.P
        self.xm = self.dram_in("xm", [NM, D])
        self.xs = self.dram_in("xs", [LS, D])
        self.ck = self.dram_in("ck", [256, 256])
        self.cv = self.dram_in("cv", [256, 256])
        self.s0f = self.dram_in("s0f", [8, 128, 128])
        self.s0b = self.dram_in("s0b", [8, 128, 128])
        self.vecs_d = self.dram_in("vecs", [128, NV])
        self.cf32_d = self.dram_in("cf32", [128, 256 + LS])
        self.cb16_d = self.dram_in("cb16", [128, 640])
        self.rope_d = self.dram_in("rope", [128, 2 * LS])
        self.ropeseg_d = self.dram_in("ropeseg", [128, 2 * SEG])
        self.finrep_d = self.dram_in("finrep", [128, D])
        self.ptab_d = self.dram_in("ptab", [128, 4 * (512 + 3 * SEG)])
        self.wb = self.dram_in("wb", [NWB, 128, 8192])
        self.y = self.dram_out("y", [NM, D])
        self.nk = self.dram_out("nk", [NP, 256])
        self.nv = self.dram_out("nv", [NP, 256])
        self.nsf = self.dram_out("nsf", [2, 8, 128, 128])
        self.nsb = self.dram_out("nsb", [2, 8, 128, 128])
        self.out_res = [Res("o_y"), Res("o_nk"), Res("o_nv"), Res("o_nsf"), Res("o_nsb")]

        self.xTp = self.sb("xTp", [128, NCH, NP])
        self.xTs_flat = self.sb("xTs", [128, NCH * SEG])
        self.xTs = self.xTs_flat[:, :].rearrange("p (a b) -> p a b", b=SEG)
        self.hT = self.sb("hT", [128, NCH, LS], BF16)
        self.mixT = self.sb("mixT", [128, NCH, NP], BF16)
        self.wring = [self.sb("wr%d" % i, [128, 16, 512], BF16) for i in range(2)]
        self.vecs = self.sb("vecs", [128, NV])
        self.cf32 = self.sb("cf32", [128, 256 + LS])
        self.cb16 = self.sb("cb16", [128, 5, 128], BF16)
        self.mod = self.sb("mod", [128, 2, 96, 2])
        self.modx = self.sb("modx", [128, 2, 2, 7, 16])
        self.lbt = self.sb("lbt", [128, 2, 4, 8])
        self.gains = self.sb("gains", [128, 4])
        self.cct = self.sb("cct", [128, 8])
        self._cc = {}
        self.sqb = [self.sb("sqb%d" % i, [128, 512]) for i in range(2)]
        self.sqh = [t[:, :].bitcast(BF16) for t in self.sqb]
        self.rstd = self.sb("rstd", [128, 512])
        self.tmpf = [self.sb("tmpf%d" % i, [128, 512]) for i in range(2)]
        self.arena = self.sb("arena", [128, 14080])
        self.pb = [self.st.enter_context(nc.psum_tensor("pb%d" % i, [128, 512], F32)) for i in range(8)]
        self.pbr = [Res("pb%d" % i) for i in range(8)]

        self.ident_f = self.cf32[:, 0:128]
        self.ones_f = self.cf32[:, 128:256]
        self.scanmask = self.cf32[:, 256:256 + LS]
        self.ident_b = self.cb16[:, 0, :]
        self.ones_b = self.cb16[:, 1, :]
        self.nmask = [self.cb16[:, 2, :], self.cb16[:, 3, :]]
        self.rotm = self.cb16[:, 4, :]

        self.r_const = Res("const")
        self.r_mod = Res("mod")
        self.r_xTp = [Res("xTp%d" % c) for c in range(NCH)]
        self.r_xTs = [Res("xTs%d" % c) for c in range(NCH)]
        self.r_hT = Res("hT")
        self.r_mix = [Res("mix%d" % c) for c in range(NCH)]
        self.wres = [Res("w0"), Res("w1")]
        self.wsem = [P.dsem("w0", "pool"), P.dsem("w1", "pool")]
        self.r_sqb = [Res("sqb0"), Res("sqb1")]
        self.r_sq4 = [Res("sq4_%d" % i) for i in range(4)]
        self.r_rstd = Res("rstd")
        self.r_tmpf = [Res("tmpf0"), Res("tmpf1")]
        self.r_ar = {}

        self.wsched = self.wsched_in
        self.req = []
        self.wpos = 0
        self.wissued = 0

        csem = P.dsem("c_sp", "sp")
        self.dma("sp", self.vecs[:, :], self.vecs_d, [], [self.r_const], csem)
        self.dma("sp", self.cf32[:, :], self.cf32_d, [], [self.r_const], csem)
        csem2 = P.dsem("c_pool", "pool")
        self.dma("pool", self.cb16[:, :, :].rearrange("p a b -> p (a b)"), self.cb16_d, [],
                 [self.r_const], csem2)

        self.phase_adaln()
        if self.stage >= 1:
            self.phase_load_main()
        if self.stage >= 2:
            self.phase_mixer(sample=False)
        if self.stage >= 3:
            self.phase_mixer(sample=True)
        if self.stage >= 4:
            self.phase_mlp(0)
        if self.stage >= 5:
            self.phase_pool()
            self.phase_mlp(1)
        if self.stage >= 6:
            self.phase_final()
        self.finish()
        return nc

    def ar(self, name, off, words, shape, dt=F32):
        v = self.arena[:, off:off + words]
        if dt == BF16:
            v = v.bitcast(BF16)
        if len(shape) == 3:
            v = v.rearrange("p (a b) -> p a b", b=shape[2])
        elif len(shape) == 4:
            v = v.rearrange("p (a b c) -> p a b c", b=shape[2], c=shape[3])
        assert tuple(v.shape) == tuple(shape), (name, v.shape, shape)
        if name not in self.r_ar:
            self.r_ar[name] = Res(name)
        return v, self.r_ar[name]

    def phase_adaln(self):
        P = self.P
        V = self.vecs
        rc = [self.r_const]
        self.ada_sg = self.sb("ada_sg", [128, 32])[:, :]
        self.ada_sT = self.sb("ada_sT", [128, 32], BF16)[:, :]
        self.r_sT = Res("ada_sT")
        r_sg = Res("ada_sg")
        self.act(self.ada_sg, V[:, V_CV:V_CV + 32], AF.Sigmoid, rc, [r_sg])
        self.tt(self.ada_sT, V[:, V_CV:V_CV + 32], self.ada_sg, ALU.mult, rc + [r_sg], [self.r_sT])
        for nb in range(8):
            self.adaln_block(0, nb)
        self.adaln_finish(0, part=0)

    def adaln_block(self, l, nb):
        self.adaln_tr(l, nb, self.adaln_mm(l, nb, 0))

    def adaln_mm(self, l, nb, bi):
        sT = self.ada_sT
        w, wr = self.wnext(WB_ADA0 + l * NB_ADA + nb)
        bk, bkr = self.bank()
        for kc in range(16):
            self.mm(bk[0:2, :], sT[:, 2 * kc:2 * kc + 2], w[:, kc, :], kc == 0, kc == 15,
                    [wr, self.r_sT], [bkr])
        return (bk, bkr, bi)

    def adaln_tr(self, l, nb, st):
        bk, bkr, bi = st
        modps, r_modps = self.pb[7], self.pbr[7]
        mt, mtr = [(self.sqb[1], self.r_sqb[1]), (self.tmpf[1], self.r_tmpf[1]), (self.rstd, self.r_rstd)][bi]
        self.act(mt[0:2, :], bk[0:2, :], AF.Copy, [bkr], [mtr])
        for j in range(4):
            ch = nb * 4 + j
            self.mm(modps[:, 2 * ch:2 * ch + 2], mt[0:2, j * 128:(j + 1) * 128],
                    self.ident_f[0:2, 0:2], True, True, [mtr, self.r_const], [r_modps])

    def adaln_finish(self, l, part=None):
        V = self.vecs
        rc = [self.r_const]
        modps, r_modps = self.pb[7], self.pbr[7]
        c0, c1 = {None: (0, 96), 0: (0, 32), 1: (32, 96)}[part]
        bada = V[:, V_BADA + l * 96 + c0:V_BADA + l * 96 + c1].unsqueeze(2).broadcast_to([128, c1 - c0, 2])
        self.tt(self.mod[:, l, c0:c1, :], modps[:, 2 * c0:2 * c1].rearrange("p (a b) -> p a b", b=2), bada,
                ALU.add, [r_modps] + rc, [self.r_mod])
        sD = float(np.sqrt(D))
        rm = [self.r_mod]
        for v in range(2):
            M = lambda j: self.mod[:, l, j * 16:(j + 1) * 16, v]
            X = lambda k: self.modx[:, l, v, k, :]
            if part in (None, 0):
                self.stt(X(0), M(1), 1.0, V[:, V_NMIX + l * 16:V_NMIX + (l + 1) * 16], ALU.add, ALU.mult,
                         rm + rc, [self.r_mod])
                self.ts(X(0), X(0), sD, None, ALU.mult, None, rm, [self.r_mod])
                self.cp(X(1), M(0), rm, [self.r_mod])
            if part in (None, 1):
                self.cp(X(2), M(2), rm, [self.r_mod])
                self.stt(X(3), M(4), 1.0, V[:, V_NMLP + l * 16:V_NMLP + (l + 1) * 16], ALU.add, ALU.mult,
                         rm + rc, [self.r_mod])
                self.ts(X(3), X(3), sD, None, ALU.mult, None, rm, [self.r_mod])
                self.cp(X(4), M(3), rm, [self.r_mod])
                self.cp(X(5), M(5), rm, [self.r_mod])
                self.tt(X(6), M(2), V[:, V_PSC:V_PSC + 16], ALU.mult, rm + rc, [self.r_mod])
        if l == 1 or part == 1:
            return
        s128 = float(np.sqrt(128.0))
        self.ts(self.gains[:, 0:3], V[:, V_QN:V_QN + 3], s128, None, ALU.mult, None, rc, [self.r_mod])
        for d in range(2):
            e3, r_e3 = self.ar("lb_e", 1100, 24, [128, 24])
            self.act(e3, V[:, V_LB + d * 24:V_LB + (d + 1) * 24], AF.Exp, rc, [r_e3])
            tmp = self.lbt[:, d, 3, :]
            self.tt(tmp, e3[:, 0:8], e3[:, 8:16], ALU.add, [r_e3], [self.r_mod])
            self.tt(tmp, tmp, e3[:, 16:24], ALU.add, [r_e3, self.r_mod], [self.r_mod])
            self.P.add("dve", lambda h, tmp=tmp: h.reciprocal(tmp, tmp), [self.r_mod], [self.r_mod])
            self.tt(self.lbt[:, d, 0, :], e3[:, 0:8], tmp, ALU.mult, [r_e3, self.r_mod], [self.r_mod])
            self.ts(self.lbt[:, d, 1, :], self.lbt[:, d, 0, :], -1.0, 1.0, ALU.mult, ALU.add,
                    [self.r_mod], [self.r_mod])
            self.act(self.lbt[:, d, 2, :], self.lbt[:, d, 1, :], AF.Ln, [self.r_mod], [self.r_mod])
        if "mod" in self.dbg:
            self.dump("d_mod", self.mod[:, :, :, :].rearrange("p a b c -> p (a b c)"), [self.r_mod], 384)
            self.dump("d_lbt", self.lbt[:, :, :, :].rearrange("p a b c -> p (a b c)"), [self.r_mod], 64)

    def dump(self, name, ap, reads, ncols, dt=F32):
        d = self.nc.dram_tensor(name, [128, ncols], dt, kind="ExternalOutput").ap()
        if len(ap.shape) == 3:
            d = d.rearrange("p (a b) -> p a b", b=ap.shape[2])
        r = Res("dbg_" + name)
        ds = self.P.dsem("dbg_" + name, "sp")
        self.dma("sp", d, ap, reads, [r], ds)
        self.out_res.append(r)
        self.dbg_out.append(name)

    def finish(self):
        self.P.pending["sp"] = [d.last for d in self.P.dsems if d.last is not None]
        self.P.add("sp", None, [], [])
        if self.wsched is not None:
            self.P.emit()
        self.st.close()


    def load_tokens(self, src, rows, dst_fn, xin, xsem, t_off=0):
        for ti, (r0, n) in enumerate(rows):
            xi, xr = xin[ti % 2]
            self.dma("sp", xi[0:n, :], src[r0:r0 + n, :], [], [xr], xsem[ti % 2])
            for cg in range(4):
                bk, bkr = self.bank()
                for j in range(4):
                    c = cg * 4 + j
                    self.tr(bk[0:128, j * 128:j * 128 + n], xi[0:n, c * 128:(c + 1) * 128],
                            self.ident_f[0:n, 0:n], [xr, self.r_const], [bkr])
                dst, dres = dst_fn(cg, r0 - t_off, n)
                srcv = bk[:, :].rearrange("p (a b) -> p a b", b=128)[:, :, 0:n]
                self.act(dst, srcv, AF.Copy, [bkr], dres)

    def xin_bufs(self):
        if not hasattr(self, "_xsem"):
            self._xsem = [self.P.dsem("xin0", "sp"), self.P.dsem("xin1", "sp")]
        xin = [self.ar("xin%d" % i, i * 2048, 2048, [128, 2048]) for i in range(2)]
        return xin, self._xsem

    def phase_load_main(self):
        self.P.barrier()
        xin, xsem = self.xin_bufs()
        rows = [(r0, 128) for r0 in range(0, NP, 128)]
        self.load_tokens(self.xm, rows,
                         lambda cg, t0, n: (self.xTp[:, cg * 4:(cg + 1) * 4, t0:t0 + n],
                                            self.r_xTp[cg * 4:(cg + 1) * 4]), xin, xsem)
        if "xT" in self.dbg:
            self.dump("d_xTp", self.xTp[:, :, :].rearrange("p a b -> p (a b)"), self.r_xTp, NCH * NP)

    def load_seg(self):
        xin, xsem = self.xin_bufs()
        rows = [(NP, 128), (NP + 128, 128), (NP + 256, 16)]
        self.load_tokens(self.xm, rows,
                         lambda cg, t0, n: (self.xTs[:, cg * 4:(cg + 1) * 4, t0:t0 + n],
                                            self.r_xTs[cg * 4:(cg + 1) * 4]), xin, xsem, t_off=NP)

    def norm_mod(self, xsrc, ranges, l, k0):
        sq4 = [(self.sqh[i // 2][:, (i % 2) * 512:(i % 2) * 512 + 512], self.r_sq4[i]) for i in range(4)]
        t4 = [(self.tmpf[0], [self.r_tmpf[0]]), (self.tmpf[1], [self.r_tmpf[1]]),
              (self.sqb[0], [self.r_sq4[0], self.r_sq4[1]]), (self.sqb[1], [self.r_sq4[2], self.r_sq4[3]])]
        for (t0, n, v, h0) in ranges:
            bk, bkr = self.bank()
            for c in range(NCH):
                xa, xr = xsrc(c, t0, n)
                sq, sqr = sq4[c % 4]
                if c % 2 == 0:
                    self.act(sq[:, 0:n], xa, AF.Square, [xr], [sqr])
                else:
                    self.tt(sq[:, 0:n], xa, xa, ALU.mult, [xr], [sqr])
                self.mm(bk[:, 0:n], self.ones_b, sq[:, 0:n], c == 0, c == NCH - 1, [sqr, self.r_const], [bkr])
            self.rsqrt(bk, D * EPS, n, bkr)
            for c in range(NCH):
                xa, xr = xsrc(c, t0, n)
                tf, tfr = t4[c % 4]
                self.stt(tf[:, 0:n], xa, self.modx[:, l, v, k0, c:c + 1], self.rstd[:, 0:n], ALU.mult, ALU.mult,
                         [xr, self.r_mod, self.r_rstd], tfr)
                self.act(self.hT[:, c, h0:h0 + n], tf[:, 0:n], AF.Identity, tfr + [self.r_mod], [self.r_hT],
                         bias=self.modx[:, l, v, k0 + 1, c:c + 1])

    def xmain(self, c, t0, n):
        if t0 < NP:
            assert t0 + n <= NP
            return self.xTp[:, c, t0:t0 + n], self.r_xTp[c]
        return self.xTs[:, c, t0 - NP:t0 - NP + n], self.r_xTs[c]

    def proj(self, w, wr, ms, tiles, rhs, rhs_res, evac, nk=16, kc0=0):
        for m in ms:
            for (t0, n) in tiles:
                bk, bkr = self.bank()
                for kc in range(nk):
                    self.mm(bk[:, 0:n], w[:, kc0 + kc, m * 128:(m + 1) * 128], rhs(kc, t0, n), kc == 0,
                            kc == nk - 1, [wr] + rhs_res, [bkr])
                evac(m, t0, n, bk, bkr)

    def proj_units(self, wfn, ms, tiles, rhs, rhs_res, evac):
        state = {}

        def unit(m, t0, n):
            if "w" not in state:
                state["w"] = wfn()
            w, wr = state["w"]
            bk, bkr = self.bank()
            for kc in range(16):
                self.mm(bk[:, 0:n], w[:, kc, m * 128:(m + 1) * 128], rhs(kc, t0, n), kc == 0, kc == 15,
                        [wr] + rhs_res, [bkr])
            evac(m, t0, n, bk, bkr)
        return [(lambda m=m, t0=t0, n=n: unit(m, t0, n)) for m in ms for (t0, n) in tiles]

    def headnorm(self, bk, bkr, n, gain_col, dst, dres, rope_cols=None, want_f32=False):
        qf, qfr = self.tmpf[0], self.r_tmpf[0]
        sq, sqr = self.sqb[0], self.r_sqb[0]
        g = self.gains[:, gain_col:gain_col + 1]
        self.act(qf[:, 0:n], bk[:, 0:n], AF.Copy, [bkr, self.r_mod], [qfr], scale=g)
        self.act(self.sqh[0][:, 0:n], bk[:, 0:n], AF.Square, [bkr], [sqr])
        b2, b2r = self.bank()
        self.mm(b2[:, 0:n], self.ones_b, self.sqh[0][:, 0:n], True, True, [sqr, self.r_const], [b2r])
        self.rsqrt(b2, 128.0 * EPS, n, b2r)
        if rope_cols is None and not want_f32:
            self.tt(dst, qf[:, 0:n], self.rstd[:, 0:n], ALU.mult, [qfr, self.r_rstd], dres)
            return None, None
        qn, qnr = self.tmpf[1], self.r_tmpf[1]
        self.tt(qn[:, 0:n], qf[:, 0:n], self.rstd[:, 0:n], ALU.mult, [qfr, self.r_rstd], [qnr])
        if rope_cols is None:
            self.act(dst, qn[:, 0:n], AF.Copy, [qnr], dres)
            return qn, qnr
        cos, sin, rr = rope_cols
        qnb, qnbr = self.ar("qnb", 8704, 256, [128, 512], BF16)
        self.act(qnb[:, 0:n], qn[:, 0:n], AF.Copy, [qnr], [qnbr])
        b3, b3r = self.bank()
        self.mm(b3[:, 0:n], self.rotm, qnb[:, 0:n], True, True, [qnbr, self.r_const], [b3r])
        t1, t1r = self.ar("ropet1", 8960, 512, [128, 512])
        self.tt(t1[:, 0:n], qn[:, 0:n], cos, ALU.mult, [qnr, rr], [t1r])
        self.tt(sq[:, 0:n], b3[:, 0:n], sin, ALU.mult, [b3r, rr], [sqr])
        self.tt(dst, t1[:, 0:n], sq[:, 0:n], ALU.add, [t1r, sqr], dres)
        return None, None

    def extract(self, dst, dres, src, sres):
        oh = lambda q: self.vecs[:, V_OH + q:V_OH + q + 1]
        self.ts(dst, src[:, 0:SEG], oh(0), None, ALU.mult, None, sres + [self.r_const], dres)
        for q in range(1, 4):
            s0 = SEG_STARTS[q]
            self.stt(dst, src[:, s0:s0 + SEG], oh(q), dst, ALU.mult, ALU.add, sres + dres + [self.r_const], dres)

    def phase_mixer(self, sample):
        P = self.P
        Lg = LS if sample else NP
        v = 1 if sample else 0
        seqs = [(0, LS)] if sample else [(0, 256), (256, 256)]
        tiles = [(t0, 512) for t0 in range(0, Lg, 512)]
        ntile = Lg // 128
        P.barrier()
        if not sample:
            self.norm_mod(lambda c, t0, n: (self.xTp[:, c, t0:t0 + n], self.r_xTp[c]), [(0, 512, 0, 0)], 0, 0)
        else:
            xin, xsem = self.xin_bufs()
            xsTs = [self.ar("xsT%d" % i, 4096 + 4096 * i, 4096, [128, 16, 256]) for i in range(2)]
            for blk in range(4):
                xsT, xsTr = xsTs[blk % 2]
                self.load_tokens(self.xs, [(blk * 256, 128), (blk * 256 + 128, 128)],
                                 lambda cg, t0, n, xsT=xsT, xsTr=xsTr: (xsT[:, cg * 4:(cg + 1) * 4, t0:t0 + n], [xsTr]),
                                 xin, xsem, t_off=blk * 256)
                if blk >= 1:
                    pT_, pTr_ = xsTs[(blk - 1) % 2]
                    self.norm_mod(lambda c, t0, n, pT_=pT_, pTr_=pTr_: (pT_[:, c, t0:t0 + n], pTr_),
                                  [(0, 256, 1, (blk - 1) * 256)], 0, 0)
            pT_, pTr_ = xsTs[3 % 2]
            self.norm_mod(lambda c, t0, n, pT_=pT_, pTr_=pTr_: (pT_[:, c, t0:t0 + n], pTr_), [(0, 256, 1, 3 * 256)], 0, 0)
        if "hT" in self.dbg:
            self.dump("d_hT%d" % v, self.hT[:, :, 0:Lg], [self.r_hT], NCH * Lg, BF16)
        if sample:
            hseg, hsegr = self.xsv("hseg", 0, 2176, [128, 16, SEG], BF16)
            for c in range(NCH):
                self.extract(hseg[:, c, :], [hsegr], self.hT[:, c, :], [self.r_hT])
        sub = int(os.environ.get("KSUB", "9"))
        if sub < 1:
            return
        P.barrier()
        hrhs = lambda kc, t0, n: self.hT[:, kc, t0:t0 + n]
        hres = [self.r_hT]
        qT, qTr = self.ar("qT", 0, 4096, [128, 8, LS], BF16)
        kT, kTr = self.ar("kT", 4096, 1280, [128, 2, 1280], BF16)
        vtok, vtokr = self.ar("vtok", 5376, 1280, [128, 10, 256], BF16)
        pT = [self.ar("pT%d" % i, 6656 + 256 * i, 256, [128, 512], BF16) for i in range(2)]
        rden, rdenr = self.ar("rden", 12032, 512, [128, 512])
        if sample:
            rope, roper = self.ar("rope", 9472, 2048, [128, 2, LS])
            ropes, ropesr = self.ar("ropeseg", 7168, 2 * SEG, [128, 2, SEG])
            ckst, ckstr = self.ar("ckst", 12544, 512, [128, 2, 256])
            cvst, cvstr = self.ar("cvst", 13056, 512, [128, 2, 256])
            rsem = P.dsem("rope", "sp")
            self.dma("sp", rope[:, :, :].rearrange("p a b -> p (a b)"), self.rope_d, [], [roper], rsem)
            self.dma("sp", ropes[:, :, :].rearrange("p a b -> p (a b)"), self.ropeseg_d, [], [ropesr], rsem)
            self.dma("sp", ckst, self.ck.rearrange("(t p) f -> p t f", p=128), [], [ckstr], rsem)
            self.dma("sp", cvst, self.cv.rearrange("(t p) f -> p t f", p=128), [], [cvstr], rsem)
        else:
            nkst, nkstr = self.ar("nkst", 9472, 1024, [128, 4, 256])
            nvst, nvstr = self.ar("nvst", 10496, 1024, [128, 4, 256])
            osem = P.dsem("nkv", "sp")

        for blk in range(2):
            w, wr = self.wnext(WB_IN + blk)

            def evq(m, t0, n, bk, bkr, blk=blk):
                hq = blk * 4 + m
                rc = (ropes[:, 0, t0:t0 + n], ropes[:, 1, t0:t0 + n], ropesr) if sample else None
                self.headnorm(bk, bkr, n, 0, qT[:, hq, t0:t0 + n], [qTr], rc)
            if sample:
                self.proj(w, wr, range(4), [(0, SEG)], lambda kc, t0, n: hseg[:, kc, t0:t0 + n], [hsegr], evq)
            else:
                self.proj(w, wr, range(4), tiles, hrhs, hres, evq)
        kss = int(os.environ.get("KSS", "9"))
        if kss < 1:
            return
        w, wr = self.wnext(WB_IN + 2)

        def evk(m, t0, n, bk, bkr):
            rc = (rope[:, 0, t0:t0 + n], rope[:, 1, t0:t0 + n], roper) if sample else None
            qn, qnr = self.headnorm(bk, bkr, n, 1, kT[:, m, t0:t0 + n], [kTr], rc, want_f32=not sample)
            if not sample:
                for tt in range(n // 128):
                    b4, b4r = self.bank()
                    self.tr(b4[:, 0:128], qn[:, tt * 128:(tt + 1) * 128], self.ident_f, [qnr, self.r_const], [b4r])
                    self.act(nkst[:, (t0 // 128) + tt, m * 128:(m + 1) * 128], b4[:, 0:128], AF.Copy, [b4r], [nkstr])
        self.proj(w, wr, range(2), tiles, hrhs, hres, evk)
        if kss < 2:
            return
        for tt in range(ntile):
            bk, bkr = self.bank()
            for kc in range(16):
                self.mm(bk[:, 0:256], self.hT[:, kc, tt * 128:(tt + 1) * 128], w[:, kc, 256:512], kc == 0, kc == 15,
                        [wr, self.r_hT], [bkr])
            if sample:
                self.act(vtok[:, tt, :], bk[:, 0:256], AF.Copy, [bkr], [vtokr])
            else:
                self.act(nvst[:, tt, :], bk[:, 0:256], AF.Copy, [bkr], [nvstr])
                self.cp(vtok[:, tt, :], nvst[:, tt, :], [nvstr], [vtokr])
        if kss < 3:
            return
        if not sample:
            self.dma("sp", self.nk.rearrange("(t p) f -> p t f", p=128), nkst, [nkstr], [self.out_res[1]], osem)
            self.dma("sp", self.nv.rearrange("(t p) f -> p t f", p=128), nvst, [nvstr], [self.out_res[2]], osem)
        else:
            for tt in range(2):
                for j in range(2):
                    b4, b4r = self.bank()
                    self.tr(b4[:, 0:128], ckst[:, tt, j * 128:(j + 1) * 128], self.ident_f, [ckstr, self.r_const], [b4r])
                    self.act(kT[:, j, LS + tt * 128:LS + (tt + 1) * 128], b4[:, 0:128], AF.Copy, [b4r], [kTr])
                self.act(vtok[:, 8 + tt, :], cvst[:, tt, :], AF.Copy, [cvstr], [vtokr])
        if "qk" in self.dbg:
            self.dump("d_qT%d" % v, qT[:, :, 0:Lg], [qTr], 8 * Lg, BF16)
            self.dump("d_kT%d" % v, kT[:, :, :].rearrange("p a b -> p (a b)"), [kTr], 2560, BF16)
            self.dump("d_vtok%d" % v, vtok[:, :, :].rearrange("p a b -> p (a b)"), [vtokr], 2560, BF16)

        if sub < 2:
            return
        scale = float(128.0 ** -0.5)
        accs = [(3, 4), (5, 6)]
        acc_i = 0
        sb_i = 0
        for si, (s0, L) in enumerate(seqs):
            if sample:
                ktiles = list(range(10))
                qblocks = [(0, SEG)]
            else:
                ktiles = [si * 2, si * 2 + 1]
                qblocks = [(s0, 256)]
            nk_ = len(ktiles)
            for j in range(2):
                for g in range(4):
                    hq = j * 4 + g
                    for (t0, n) in qblocks:
                        oi, di = accs[acc_i % 2]
                        acc_i += 1
                        ob, obr, db, dbr = self.pb[oi], self.pbr[oi], self.pb[di], self.pbr[di]
                        sbk = [None] * nk_

                        def issue_s(ki):
                            nonlocal sb_i
                            bi = sb_i % 3
                            sb_i += 1
                            kt = ktiles[ki]
                            self.mm(self.pb[bi][:, 0:n], kT[:, j, kt * 128:kt * 128 + 128], qT[:, hq, t0:t0 + n], True, True,
                                    [kTr, qTr], [self.pbr[bi]])
                            sbk[ki] = bi
                        issue_s(0)
                        if nk_ > 1:
                            issue_s(1)
                        for ki, kt in enumerate(ktiles):
                            if ki + 2 < nk_:
                                issue_s(ki + 2)
                            bi = sbk[ki]
                            pt, ptr = pT[ki % 2]
                            self.act(pt[:, 0:n], self.pb[bi][:, 0:n], AF.Exp, [self.pbr[bi]], [ptr], scale=scale)
                            first, last = ki == 0, ki == nk_ - 1
                            self.mm(ob[:, 0:n], vtok[:, kt, j * 128:(j + 1) * 128], pt[:, 0:n], first, last,
                                    [vtokr, ptr], [obr])
                            self.mm(db[:, 0:n], self.ones_b, pt[:, 0:n], first, last, [ptr, self.r_const], [dbr])
                        self.P.add("dve", lambda h, n=n, db=db: h.reciprocal(rden[:, 0:n], db[:, 0:n]), [dbr], [rdenr])
                        self.tt(self.mixT[:, hq, t0:t0 + n], ob[:, 0:n], rden[:, 0:n], ALU.mult, [obr, rdenr],
                                [self.r_mix[hq]])
        if "att" in self.dbg:
            self.dump("d_att%d" % v, self.mixT[:, 0:8, :].rearrange("p a b -> p (a b)"), self.r_mix[0:8], 8 * NP, BF16)
        if sub < 3:
            return
        self.hgrn(sample, Lg, seqs, tiles, ntile, hrhs, hres)
        if sub < 4:
            return
        P.barrier()
        if sample:
            self.load_seg()
            otiles = [(0, SEG)]
        else:
            otiles = [(0, NP)]
        mrhs = lambda kc, t0, n: self.mixT[:, kc, t0:t0 + n]
        for cb in range(4):
            w, wr = self.wnext(WB_OUT + cb)

            def evo(m, t0, n, bk, bkr, cb=cb):
                c = cb * 4 + m
                if sample:
                    xa, xr = self.xTs[:, c, t0:t0 + n], self.r_xTs[c]
                else:
                    xa, xr = self.xTp[:, c, t0:t0 + n], self.r_xTp[c]
                self.stt(xa, bk[:, 0:n], self.modx[:, 0, v, 2, c:c + 1], xa, ALU.mult, ALU.add,
                         [bkr, xr, self.r_mod], [xr])
            self.proj(w, wr, range(4), otiles, mrhs, self.r_mix, evo)
        if "x1" in self.dbg:
            if sample:
                self.dump("d_x1s", self.xTs_flat[:, :], self.r_xTs, NCH * SEG)
            else:
                self.dump("d_x1p", self.xTp[:, :, :].rearrange("p a b -> p (a b)"), self.r_xTp, NCH * NP)

    def xsv(self, name, off, words, shape, dt=F32):
        v = self.xTs_flat[:, off:off + words]
        if dt == BF16:
            v = v.bitcast(BF16)
        if len(shape) == 3:
            v = v.rearrange("p (a b) -> p a b", b=shape[2])
        elif len(shape) == 4:
            v = v.rearrange("p (a b c) -> p a b c", b=shape[2], c=shape[3])
        assert tuple(v.shape) == tuple(shape), (name, v.shape, shape)
        if name not in self.r_ar:
            self.r_ar[name] = Res(name)
        return v, self.r_ar[name]

    def hgrn(self, sample, Lg, seqs, tiles, ntile, hrhs, hres):
        P = self.P
        P.barrier()
        v = 1 if sample else 0
        nch = Lg // CH
        gsT, gsTr = self.ar("gsT", 0, 2048, [128, 4, LS], BF16)
        qs, qsr = self.ar("qs", 2048, 1024, [128, LS])
        sgd = [self.ar("sgf", 3072, 1024, [128, LS]), self.ar("sgb", 4096, 1024, [128, LS])]
        hiT, hiTr = self.ar("hiT", 5120, 512, [128, LS], BF16)
        itok, itokr = self.ar("itok", 5632, 512, [128, 8, 128], BF16)
        A, Ar = self.ar("hA", 6144, 1024, [128, LS])
        Bb, Br = self.ar("hB", 7168, 1024, [128, LS])
        C, Cr = self.ar("hC", 8192, 1024, [128, LS])
        Qt = [self.ar("Qt%d" % d, 9216 + 512 * d, 512, [128, LS], BF16) for d in range(2)]
        Kt = [self.ar("Kt%d" % d, 10240 + 512 * d, 512, [128, LS], BF16) for d in range(2)]
        Ktok = [[self.ar("Ktok%d%d" % (d, i), 11264 + 64 * (2 * d + i), 64, [128, 128], BF16) for i in range(2)]
                for d in range(2)]
        Sp = [self.ar("Sp%d" % d, 11520 + 1024 * d, 1024, [128, 16, 128], BF16) for d in range(2)]
        attm = [self.xsv("attm%d" % d, 512 * d, 512, [128, 8, 128], BF16) for d in range(2)]
        Sb = [[self.xsv("S%d%d" % (d, i), 1024 + 128 * (2 * d + i), 128, [128, 128]) for i in range(2)]
              for d in range(2)]
        dtmp = [[self.xsv("dt%d%d" % (d, i), 1536 + 128 * (2 * d + i), 128, [128, 128]) for i in range(2)]
                for d in range(2)]
        oT, oTr = self.xsv("oT", 2048, 512, [128, 512])
        scal, scalr = self.xsv("scal", 2560, 160, [128, 2, 5, 16])
        sgq, sgqr = self.xsv("sgq", 2720, 512, [128, 512])
        otmp, otmpr = self.xsv("otmp", 3232, 512, [128, LS], BF16)
        if not hasattr(self, "_ssem"):
            self._ssem = [P.dsem("s0ld%d" % d, "sp") for d in range(2)]
            self._osem = [P.dsem("sout%d" % d, "sp") for d in range(2)]
        ob, obr = self.pb[6], self.pbr[6]

        itoks = [(itok, itokr), self.ar("itok2", 13568, 512, [128, 8, 128], BF16)]

        def emit_hg(half):
            w, wr = self.wnext(WB_IN + 3 + half)

            def evg(m, t0, n, bk, bkr):
                self.act(sgq[:, 0:n], bk[:, 0:n], AF.Sigmoid, [bkr], [sgqr])
                self.tt(gsT[:, m, t0:t0 + n], bk[:, 0:n], sgq[:, 0:n], ALU.mult, [bkr, sgqr], [gsTr])
            self.proj(w, wr, range(4), tiles, hrhs, hres, evg)

        def proj_head_units(h):
            def evh(m, t0, n, bk, bkr):
                if m == 0:
                    self.act(sgq[:, 0:n], bk[:, 0:n], AF.Sigmoid, [bkr], [sgqr])
                    self.tt(qs[:, t0:t0 + n], bk[:, 0:n], sgq[:, 0:n], ALU.mult, [bkr, sgqr], [qsr])
                elif m in (1, 2):
                    sg, sgr = sgd[m - 1]
                    self.act(sg[:, t0:t0 + n], bk[:, 0:n], AF.Sigmoid, [bkr], [sgr])
                else:
                    self.act(hiT[:, t0:t0 + n], bk[:, 0:n], AF.Copy, [bkr], [hiTr])
            units = self.proj_units(lambda: self.wnext(WB_IN + 5 + h), range(4), tiles, hrhs, hres, evh)

            def itok_unit():
                it, itr = itoks[h % 2]
                for tt in range(ntile):
                    bk, bkr = self.bank()
                    bkb = bk[:, 0:64].bitcast(BF16)
                    self.tr(bkb, hiT[:, tt * 128:(tt + 1) * 128], self.ident_b, [hiTr, self.r_const], [bkr])
                    self.act(it[:, tt, :], bkb, AF.Copy, [bkr], [itr])
            return units + [itok_unit]

        def emit_prep(h):
            rm = [self.r_mod]
            for d in range(2):
                sg, sgr = sgd[d]
                lb = self.lbt[:, d, 0, h:h + 1]
                oml = self.lbt[:, d, 1, h:h + 1]
                lnoml = self.lbt[:, d, 2, h:h + 1]
                self.act(A[:, 0:Lg], sg[:, 0:Lg], AF.Ln, [sgr] + rm, [Ar], bias=lb, scale=oml)
                self.P.add("dve", lambda hd: hd.tensor_tensor_scan(Bb[:, 0:Lg], self.scanmask[:, 0:Lg], A[:, 0:Lg],
                                                                   0.0, ALU.mult, ALU.add),
                           [Ar, self.r_const], [Br])
                B3 = Bb[:, 0:Lg].rearrange("p (a b) -> p a b", b=CH)
                S = lambda k, d=d: scal[:, d, k, 0:nch]
                self.cp(S(0).unsqueeze(2), B3[:, :, 31:32], [Br], [scalr])
                self.tt(B3, B3, S(0).unsqueeze(2).broadcast_to([128, nch, CH]), ALU.subtract, [Br, scalr], [Br])
                self.cp(S(1).unsqueeze(2), B3[:, :, 63:64], [Br], [scalr])
                if d == 1:
                    self.tt(Bb[:, 0:Lg], Bb[:, 0:Lg], A[:, 0:Lg], ALU.subtract, [Br, Ar], [Br])
                self.tt(S(2), S(0), S(1), ALU.add, [scalr], [scalr])
                self.act(S(2), S(2), AF.Exp, [scalr], [scalr])
                self.act(S(3), S(0), AF.Exp, [scalr], [scalr])
                self.act(S(4), S(1), AF.Exp, [scalr], [scalr])
                self.ts(S(0), S(3), -1.0, None, ALU.mult, None, [scalr], [scalr])
                self.ts(S(1), S(4), -1.0, None, ALU.mult, None, [scalr], [scalr])
                if d == 0:
                    self.act(A[:, 0:Lg], Bb[:, 0:Lg], AF.Exp, [Br], [Ar])
                    self.act(C[:, 0:Lg], Bb[:, 0:Lg], AF.Exp, [Br] + rm, [Cr], bias=lnoml, scale=-1.0)
                else:
                    self.act(A[:, 0:Lg], Bb[:, 0:Lg], AF.Exp, [Br], [Ar], scale=-1.0)
                    self.act(C[:, 0:Lg], Bb[:, 0:Lg], AF.Exp, [Br] + rm, [Cr], bias=lnoml)
                qt, qtr = Qt[d]
                kt_, ktr = Kt[d]
                self.tt(qt[:, 0:Lg], qs[:, 0:Lg], A[:, 0:Lg], ALU.mult, [qsr, Ar], [qtr])
                self.stt(kt_[:, 0:Lg], sg[:, 0:Lg], -1.0, C[:, 0:Lg], ALU.add, ALU.mult, [sgr, Cr], [ktr])

        def emit_chain_o(h, hh, fill):
            it, itr = itoks[h % 2]
            kSD = [(3, 1), (4, 0)]
            for si, (s0, L) in enumerate(seqs):
                tl = list(range(s0 // 128, (s0 + L) // 128))
                order = [tl, tl[::-1]]
                Scur = []
                for d in range(2):
                    Sc, Scr = Sb[d][0]
                    if sample:
                        srcd = (self.s0f if d == 0 else self.s0b)[h]
                        self.dma("sp", Sc, srcd, [], [Scr], self._ssem[d])
                    else:
                        self.P.add("dve", lambda hd, Sc=Sc: hd.memset(Sc, 0.0), [], [Scr])
                    Scur.append(0)
                for idx in range(len(tl)):
                    for d in range(2):
                        tt = order[d][idx]
                        qt, qtr = Qt[d]
                        kt_, ktr = Kt[d]
                        am, amr = attm[d]
                        bk2, bk2r = self.bank()
                        self.mm(bk2[:, 0:128], kt_[:, tt * 128:(tt + 1) * 128], qt[:, tt * 128:(tt + 1) * 128], True, True,
                                [ktr, qtr], [bk2r])
                        self.tt(am[:, tt, :], bk2[:, 0:128], self.nmask[d], ALU.mult, [bk2r, self.r_const], [amr])
                c_lo, c_n = s0 // CH, L // CH
                for d in range(2):
                    kt_, ktr = Kt[d]
                    kS, kD = kSD[d]
                    k3 = kt_[:, s0:s0 + L].rearrange("p (a b) -> p a b", b=CH)
                    self.tt(k3, k3, scal[:, d, kD, c_lo:c_lo + c_n].unsqueeze(2).broadcast_to([128, c_n, CH]),
                            ALU.mult, [ktr, scalr], [ktr])
                for idx in range(len(tl)):
                    dsb = {}
                    for d in range(2):
                        tt = order[d][idx]
                        kt_, ktr = Kt[d]
                        bk, bkr = self.bank()
                        bkb = bk[:, 0:64].bitcast(BF16)
                        self.tr(bkb, kt_[:, tt * 128:(tt + 1) * 128], self.ident_b, [ktr, self.r_const], [bkr])
                        ktk, ktkr = Ktok[d][tt % 2]
                        self.cp(ktk, bkb, [bkr], [ktkr])
                        for jj in ((0, 1) if d == 0 else (1, 0)):
                            c = 2 * tt + jj
                            bk3, bk3r = self.bank()
                            self.mm(bk3[:, 0:128], ktk[64 * jj:64 * jj + 64, :], it[64 * jj:64 * jj + 64, tt, :], True, True,
                                    [ktkr, itr], [bk3r])
                            dsb[(d, c)] = (bk3, bk3r)
                    for step in range(2):
                        for d in range(2):
                            tt = order[d][idx]
                            jj = ((0, 1) if d == 0 else (1, 0))[step]
                            c = 2 * tt + jj
                            kS, kD = kSD[d]
                            sp, spr = Sp[d]
                            Sc, Scr = Sb[d][Scur[d]]
                            Sn, Snr = Sb[d][1 - Scur[d]]
                            bk3, bk3r = dsb[(d, c)]
                            self.act(sp[:, c, :], Sc, AF.Copy, [Scr, scalr], [spr], scale=scal[:, d, kS, c:c + 1])
                            self.stt(Sn, Sc, scal[:, d, 2, c:c + 1], bk3[:, 0:128], ALU.mult, ALU.add, [Scr, scalr, bk3r], [Snr])
                            Scur[d] = 1 - Scur[d]
                    for _ in range(fill_per_idx):
                        if fill:
                            fill.pop(0)()
                if not sample:
                    for d in range(2):
                        Sc, Scr = Sb[d][Scur[d]]
                        dst = (self.nsf if d == 0 else self.nsb)[si, h]
                        self.dma("sp", dst, Sc, [Scr], [self.out_res[3 + d]], self._osem[d])
            for gi in range(0, ntile, 4):
                tts = list(range(gi, min(gi + 4, ntile)))
                for tt in tts:
                    c0 = (tt - gi) * 128
                    ops = []
                    for d in range(2):
                        ops.append((it[:, tt, :], attm[d][0][:, tt, :], c0, 128, [itr, attm[d][1]]))
                        for jj in range(2):
                            c = 2 * tt + jj
                            ops.append((Sp[d][0][:, c, :], Qt[d][0][:, c * CH:(c + 1) * CH], c0 + CH * jj, CH,
                                        [Sp[d][1], Qt[d][1]]))
                    for i, (l_, r_, cc, ww, rr) in enumerate(ops):
                        self.mm(ob[:, cc:cc + ww], l_, r_, i == 0, i == len(ops) - 1, rr, [obr])
                n = len(tts) * 128
                t0 = gi * 128
                self.act(oT[:, 0:n], ob[:, 0:n], AF.Copy, [obr, self.r_mod], [oTr], scale=self.gains[:, 2:3])
                sq, sqr = self.sqb[0], self.r_sqb[0]
                self.act(self.sqh[0][:, 0:n], ob[:, 0:n], AF.Square, [obr], [sqr])
                b2, b2r = self.bank()
                self.mm(b2[:, 0:n], self.ones_b, self.sqh[0][:, 0:n], True, True, [sqr, self.r_const], [b2r])
                self.rsqrt(b2, 128.0 * EPS, n, b2r)
                tf, tfr = self.tmpf[0], self.r_tmpf[0]
                self.tt(tf[:, 0:n], oT[:, 0:n], self.rstd[:, 0:n], ALU.mult, [oTr, self.r_rstd], [tfr])
                if sample:
                    self.tt(otmp[:, t0:t0 + n], tf[:, 0:n], gsT[:, hh, t0:t0 + n], ALU.mult, [tfr, gsTr], [otmpr])
                else:
                    self.tt(self.mixT[:, 8 + h, t0:t0 + n], tf[:, 0:n], gsT[:, hh, t0:t0 + n], ALU.mult, [tfr, gsTr],
                            [self.r_mix[8 + h]])
            if sample:
                self.extract(self.mixT[:, 8 + h, 0:SEG], [self.r_mix[8 + h]], otmp, [otmpr])

        n_idx = sum(L // 128 for (_, L) in seqs)
        fill_per_idx = 2 if sample else 2
        for half in range(2):
            emit_hg(half)
            for u in proj_head_units(half * 4):
                u()
            emit_prep(half * 4)
            for hh in range(4):
                h = half * 4 + hh
                fill = []
                if hh < 3:
                    fill += proj_head_units(h + 1)
                emit_chain_o(h, hh, fill)
                while fill:
                    fill.pop(0)()
                if hh < 3:
                    emit_prep(h + 1)
                if sample:
                    blks = [(1, h * 3 + k) for k in range(3)]
                else:
                    blks = [(0, 8 + h * 2 + k) for k in range(2)]
                sts = [self.adaln_mm(l_, nb_, k) for k, (l_, nb_) in enumerate(blks)]
                for k, (l_, nb_) in enumerate(blks):
                    self.adaln_tr(l_, nb_, sts[k])
        if sample:
            self.adaln_finish(1)
        else:
            self.adaln_finish(0, part=1)
        if "hg" in self.dbg:
            self.dump("d_hg%d" % v, self.mixT[:, 8:16, :].rearrange("p a b -> p (a b)"), self.r_mix[8:16], 8 * NP, BF16)

    MAIN_TILES = [(0, NP), (NP, SEG)]

    def phase_mlp(self, l):
        P = self.P
        P.barrier()
        self.norm_mod(self.xmain, [(0, NP, 0, 0), (NP, SEG, 1, NP)], l, 3)
        P.barrier()
        uT = [self.ar("uT%d" % i, i * 6272, 6272, [128, 16, NM], BF16) for i in range(2)]
        hrhs = lambda kc, t0, n: self.hT[:, kc, t0:t0 + n]
        for s in range(4):
            u, ur = uT[s % 2]
            for j in range(4):
                w, wr = self.wnext((WB_MLP0 if l == 0 else WB_MLP1) + s * 8 + j)

                def ev1(m, t0, n, bk, bkr, j=j):
                    tf, tfr = self.tmpf[m % 2], self.r_tmpf[m % 2]
                    self.act(tf[:, 0:n], bk[:, 0:n], AF.Relu, [bkr], [tfr])
                    self.tt(u[:, j * 4 + m, t0:t0 + n], tf[:, 0:n], tf[:, 0:n], ALU.mult, [tfr], [ur])
                self.proj(w, wr, range(4), self.MAIN_TILES, hrhs, [self.r_hT], ev1)
            urhs = lambda kc, t0, n, u=u: u[:, kc, t0:t0 + n]
            for cb in range(4):
                w, wr = self.wnext((WB_MLP0 if l == 0 else WB_MLP1) + s * 8 + 4 + cb)

                def ev2(m, t0, n, bk, bkr, cb=cb):
                    c = cb * 4 + m
                    v = 0 if t0 < NP else 1
                    xa, xr = self.xmain(c, t0, n)
                    self.stt(xa, bk[:, 0:n], self.modx[:, l, v, 5, c:c + 1], xa, ALU.mult, ALU.add,
                             [bkr, xr, self.r_mod], [xr])
                self.proj(w, wr, range(4), self.MAIN_TILES, urhs, [ur], ev2)
        if "x2" in self.dbg and l == 0:
            self.dump("d_x2p", self.xTp[:, :, :].rearrange("p a b -> p (a b)"), self.r_xTp, NCH * NP)
            self.dump("d_x2s", self.xTs_flat[:, :], self.r_xTs, NCH * SEG)

    def phase_pool(self):
        P = self.P
        P.barrier()
        self.norm_mod(self.xmain, [(0, NP, 0, 0), (NP, SEG, 1, NP)], 1, 0)
        P.barrier()
        PW = 512 + 3 * SEG
        ptab, ptabr = self.ar("ptab", 0, 2 * PW, [128, 4, PW], BF16)
        psem = P.dsem("ptab", "pool")
        self.dma("pool", ptab[:, :, :].rearrange("p a b -> p (a b)"), self.ptab_d, [], [ptabr], psem)
        z = [self.ar("z%d" % i, 2 * PW + i * 1792, 1792, [128, 7, 512], BF16) for i in range(2)]
        ttiles = [(t0, 128) for t0 in range(0, NP, 128)] + [(NP, 128), (NP + 128, 128), (NP + 256, 16)]
        w, wr = self.wnext(WB_POOL)

        def emit_z(g):
            zz, zr = z[g % 2]
            for ti, (t0, n) in enumerate(ttiles):
                bk, bkr = self.bank()
                for kc in range(4):
                    self.mm(bk[0:n, :], self.hT[:, g * 4 + kc, t0:t0 + n], w[:, g * 4 + kc, :], kc == 0, kc == 3,
                            [wr, self.r_hT], [bkr])
                self.act(zz[0:n, ti, :], bk[0:n, :], AF.Copy, [bkr], [zr])

        emit_z(0)
        for g in range(4):
            zz, zr = z[g % 2]
            if g + 1 < 4:
                emit_z(g + 1)
            for m in range(4):
                c = g * 4 + m
                for s in range(2):
                    bk, bkr = self.bank()
                    for ti in range(2):
                        self.mm(bk[:, 0:256], zz[:, s * 2 + ti, m * 128:(m + 1) * 128], ptab[:, g, ti * 256:(ti + 1) * 256],
                                ti == 0, ti == 1, [zr, ptabr], [bkr])
                    xa, xr = self.xTp[:, c, s * 256:(s + 1) * 256], self.r_xTp[c]
                    self.stt(xa, bk[:, 0:256], self.modx[:, 1, 0, 6, c:c + 1], xa, ALU.mult, ALU.add,
                             [bkr, xr, self.r_mod], [xr])
                bk, bkr = self.bank()
                for ti in range(3):
                    n = 128 if ti < 2 else 16
                    self.mm(bk[:, 0:SEG], zz[0:n, 4 + ti, m * 128:(m + 1) * 128],
                            ptab[0:n, g, 512 + ti * SEG:512 + (ti + 1) * SEG], ti == 0, ti == 2, [zr, ptabr], [bkr])
                xa, xr = self.xTs[:, c, :], self.r_xTs[c]
                self.stt(xa, bk[:, 0:SEG], self.modx[:, 1, 1, 6, c:c + 1], xa, ALU.mult, ALU.add,
                         [bkr, xr, self.r_mod], [xr])
        if "x3" in self.dbg:
            self.dump("d_x3p", self.xTp[:, :, :].rearrange("p a b -> p (a b)"), self.r_xTp, NCH * NP)
            self.dump("d_x3s", self.xTs_flat[:, :], self.r_xTs, NCH * SEG)

    def phase_final(self):
        P = self.P
        P.barrier()
        fint, fintr = self.ar("fint", 0, 2048, [128, 2048])
        fsem = P.dsem("fint", "sp")
        self.dma("sp", fint, self.finrep_d, [], [fintr], fsem)
        yst = [self.ar("yst%d" % i, 2048 + i * 2048, 2048, [128, 2048]) for i in range(2)]
        ssq, ssqr = self.ar("ssq", 6144, 16, [128, 16])
        junk, junkr = self.sqh[0], self.r_sqb[0]
        ysem = [P.dsem("yo0", "sp"), P.dsem("yo1", "sp")]
        ttiles = [(t0, 128) for t0 in range(0, NP, 128)] + [(NP, 128), (NP + 128, 128), (NP + 256, 16)]
        for ti, (t0, n) in enumerate(ttiles):
            ys, ysr = yst[ti % 2]
            sc = ssq[:, (ti % 2) * 8:(ti % 2) * 8 + 8]
            for cg in range(4):
                b2, b2r = self.bank()
                for j in range(4):
                    c = cg * 4 + j
                    xa, xr = self.xmain(c, t0, n)
                    self.tr(b2[0:n, j * 128:(j + 1) * 128], xa, self.ident_f, [xr, self.r_const], [b2r])
                self.P.add("act", lambda h, b2=b2, n=n, cg=cg, sc=sc: h.activation(
                    junk[0:n, 0:512], b2[0:n, :], AF.Square, accum_out=sc[0:n, cg:cg + 1]), [b2r], [junkr, ssqr])
                self.act(ys[0:n, cg * 512:(cg + 1) * 512], b2[0:n, :], AF.Copy, [b2r], [ysr])
            self.tt(sc[0:n, 4:5], sc[0:n, 0:1], sc[0:n, 1:2], ALU.add, [ssqr], [ssqr])
            self.tt(sc[0:n, 5:6], sc[0:n, 2:3], sc[0:n, 3:4], ALU.add, [ssqr], [ssqr])
            self.tt(sc[0:n, 6:7], sc[0:n, 4:5], sc[0:n, 5:6], ALU.add, [ssqr], [ssqr])
            self.act(sc[0:n, 7:8], sc[0:n, 6:7], AF.Ln, [ssqr, self.r_mod], [ssqr], bias=self.cconst(EPS)[0:n, :],
                     scale=1.0 / D)
            self.act(sc[0:n, 7:8], sc[0:n, 7:8], AF.Exp, [ssqr], [ssqr], scale=-0.5)
            self.stt(ys[0:n, :], ys[0:n, :], sc[0:n, 7:8], fint[0:n, :], ALU.mult, ALU.mult, [ysr, ssqr, fintr], [ysr])
            self.dma("sp", self.y[t0:t0 + n, :], ys[0:n, :], [ysr], [self.out_res[0]], ysem[ti % 2])

_CACHE = {}


def _run(inputs, stage=99, dbg=(), cores=tuple(range(8))):
    f = lambda k: np.asarray(inputs[k], np.float32)
    x_prompt, x_sample = f("x_prompt"), f("x_sample")
    cache_k, cache_v = f("cache_k"), f("cache_v")
    sf, sbw = f("state_hgrn_fwd"), f("state_hgrn_bwd")
    wb = _build_weight_blocks(f("w_ada"), f("w_in_ab"), f("w_out_ab"), f("w_pool"), f("w_mlp_in"),
                              f("w_mlp_out"))
    cf32 = _const_f32()
    cb16 = _const_b16()
    rope = _rope_tables().reshape(128, 2 * LS)
    finrep = np.ascontiguousarray(np.broadcast_to(f("final_norm")[None, :], (128, D)))
    in_maps = []
    for core in cores:
        b, q = core // 4, core % 4
        s0 = SEG_STARTS[q]
        xm = np.concatenate([x_prompt[2 * core].reshape(256, D), x_prompt[2 * core + 1].reshape(256, D),
                             x_sample[b, s0:s0 + SEG]], axis=0)
        in_maps.append({
            "xm": np.ascontiguousarray(xm),
            "xs": np.ascontiguousarray(x_sample[b]),
            "ck": np.ascontiguousarray(cache_k[b, 0].reshape(256, 256)),
            "cv": np.ascontiguousarray(cache_v[b, 0].reshape(256, 256)),
            "s0f": np.ascontiguousarray(sf[b, 0]),
            "s0b": np.ascontiguousarray(sbw[b, 0]),
            "vecs": _build_vecs(core, f("c"), f("c_ctx"), f("b_ada"), f("norm_mix"), f("norm_mlp"),
                                f("q_norm"), f("k_norm"), f("hg_norm"), f("lb_raw"), f("pool_scale"),
                                f("final_norm")),
            "cf32": cf32, "cb16": cb16, "rope": rope, "ptab": _ptab(q), "wb": wb,
            "finrep": finrep,
            "ropeseg": np.ascontiguousarray(rope.reshape(128, 2, LS)[:, :, s0:s0 + SEG]).reshape(128, 2 * SEG),
        })
    B0 = Builder(stage=stage, dbg=dbg)
    B0.build()
    B = Builder(stage=stage, dbg=dbg, wsched=list(B0.req))
    nc = B.build()
    res = run_bass_kernel_spmd(nc, in_maps, core_ids=list(range(len(cores))))
    return B, res


def kernel(**inputs):
    B, res = _run(inputs)
    R = res.results
    y_prompt = np.zeros((16, 256, D), np.float32)
    y_sample = np.zeros((2, LS, D), np.float32)
    new_k = np.zeros((16, 1, 256, 2, 128), np.float32)
    new_v = np.zeros((16, 1, 256, 2, 128), np.float32)
    nsf = np.zeros((16, 1, 8, 128, 128), np.float32)
    nsb = np.zeros((16, 1, 8, 128, 128), np.float32)
    for core in range(8):
        r = R[core]
        b, q = core // 4, core % 4
        y = r["y"]
        y_prompt[2 * core] = y[0:256]
        y_prompt[2 * core + 1] = y[256:512]
        off = q * 256 - SEG_STARTS[q]
        y_sample[b, q * 256:(q + 1) * 256] = y[512 + off:512 + off + 256]
        for s in range(2):
            new_k[2 * core + s, 0] = r["nk"][s * 256:(s + 1) * 256].reshape(256, 2, 128)
            new_v[2 * core + s, 0] = r["nv"][s * 256:(s + 1) * 256].reshape(256, 2, 128)
            nsf[2 * core + s, 0] = r["nsf"][s]
            nsb[2 * core + s, 0] = r["nsb"][s]
    return (y_prompt, y_sample, new_k, new_v, nsf, nsb)
```

```python
import os
from contextlib import ExitStack

import numpy as np
import concourse.bass as bass
import concourse.mybir as mybir
from concourse.bass_utils import run_bass_kernel_spmd

F32 = mybir.dt.float32
BF16 = mybir.dt.bfloat16
AF = mybir.ActivationFunctionType
ALU = mybir.AluOpType

D = 2048
NCH = 16
NP = 512
SEG = 272
NM = NP + SEG
LS = 1024
EPS = 1e-6
SEG_STARTS = (0, 248, 504, 752)
CH = 64
SAME_ENG_SYNC = True

V_NMIX, V_NMLP, V_FIN, V_PSC, V_BADA, V_QN, V_KN, V_HN, V_LB, V_CV, V_OH, NV = (
    0, 32, 64, 80, 96, 288, 289, 290, 291, 339, 371, 376)

NB_ADA = 24
WB_ADA0 = 0
WB_ADA1 = 24
WB_IN = 48
WB_OUT = 61
WB_MLP0 = 65
WB_POOL = 97
WB_MLP1 = 98
NWB = 130


class Res:
    __slots__ = ("name", "w", "r")

    def __init__(self, name):
        self.name = name
        self.w = None
        self.r = {}


class DSem:
    def __init__(self, name, eng):
        self.name = name
        self.eng = eng
        self.n = 0
        self.h = None
        self.res = Res("sem_" + name)
        self.last = None


class Op:
    __slots__ = ("eng", "fn", "dsem", "dval", "waits", "signal", "sigval", "idx")


class Prog:
    ENGS = ("pe", "act", "dve", "pool", "sp")
    BLK = {"pe": "tensor", "act": "scalar", "dve": "vector", "pool": "gpsimd", "sp": "sync"}

    def __init__(self, nc):
        self.nc = nc
        self.ops = []
        self.dsems = []
        self.last = {e: None for e in self.ENGS}
        self.pending = {e: None for e in self.ENGS}

    def dsem(self, name, eng):
        d = DSem(name, eng)
        self.dsems.append(d)
        return d

    def barrier(self):
        snap = [p for p in self.last.values() if p is not None]
        snap += [d.last for d in self.dsems if d.last is not None]
        for e in self.ENGS:
            self.pending[e] = list(snap) + (self.pending[e] or [])

    def add(self, eng, fn, reads=(), writes=(), dsem=None):
        op = Op()
        op.eng, op.fn, op.dsem = eng, fn, dsem
        op.signal, op.sigval, op.dval = False, 0, 0
        op.idx = len(self.ops)
        deps = {}
        raw = set()

        def dep(p):
            if p is not None:
                deps[p.idx] = p

        for r in reads:
            dep(r.w)
            if r.w is not None:
                raw.add(r.w.idx)
            if r.name.startswith("pb"):
                for k_, q in r.r.items():
                    if k_ != eng:
                        dep(q)
        for w in writes:
            dep(w.w)
            for q in w.r.values():
                dep(q)
        if self.pending[eng]:
            for p in self.pending[eng]:
                dep(p)
            self.pending[eng] = None
        if dsem is not None:
            assert dsem.eng == eng
            for q in dsem.res.r.values():
                dep(q)
        rkey = eng if dsem is None else ("d", dsem.name)
        waits = []
        for p in deps.values():
            if p.dsem is not None:
                ds = p.dsem
                waits.append(("d", ds, ds.n * 16))
                ds.res.r[rkey] = op
            else:
                if p.eng == eng and (eng == "pe" or not SAME_ENG_SYNC or p.idx not in raw):
                    continue
                p.signal = True
                waits.append(("c", p, 0))
        op.waits = waits
        if dsem is not None:
            dsem.n += 1
            op.dval = dsem.n * 16
            dsem.res.r = {}
            dsem.last = op
        for r in reads:
            r.r[rkey] = op
        for w in writes:
            w.w = op
            w.r = {}
        self.ops.append(op)
        self.last[eng] = op
        return op


    def check_deadlock(self):
        per = {e: [o for o in self.ops if o.eng == e] for e in self.ENGS}
        for e in self.ENGS:
            c = 0
            for o in per[e]:
                if o.signal and o.dsem is None:
                    c += 1
                    o.sigval = c
        val = {}
        pos = {e: 0 for e in self.ENGS}
        progress = True
        while progress:
            progress = False
            for e in self.ENGS:
                while pos[e] < len(per[e]):
                    o = per[e][pos[e]]
                    ok = True
                    for kind, obj, v in o.waits:
                        if kind == "d":
                            if val.get(("d", obj.name), 0) < v:
                                ok = False
                        else:
                            if val.get(("c", obj.eng), 0) < obj.sigval:
                                ok = False
                    if not ok:
                        break
                    if o.dsem is not None:
                        val[("d", o.dsem.name)] = val.get(("d", o.dsem.name), 0) + 16
                    elif o.signal:
                        val[("c", e)] = val.get(("c", e), 0) + 1
                    pos[e] += 1
                    progress = True
        stuck = {e: (pos[e], len(per[e])) for e in self.ENGS if pos[e] < len(per[e])}
        return stuck, {e: len(per[e]) for e in self.ENGS}, val

    def emit(self):
        nc = self.nc
        per = {e: [o for o in self.ops if o.eng == e] for e in self.ENGS}
        for e in self.ENGS:
            c = 0
            for o in per[e]:
                if o.signal and o.dsem is None:
                    c += 1
                    o.sigval = c
        with ExitStack() as st:
            esem = {e: st.enter_context(nc.semaphore("es_" + e)) for e in self.ENGS}
            for d in self.dsems:
                d.h = st.enter_context(nc.semaphore("ds_" + d.name))
            block = st.enter_context(nc.Block())

            def run(e, h):
                known = {}
                for o in per[e]:
                    for kind, obj, val in o.waits:
                        if kind == "d":
                            key, sem, v = ("d", obj.name), obj.h, val
                        else:
                            key, sem, v = ("c", obj.eng), esem[obj.eng], obj.sigval
                        if known.get(key, 0) >= v:
                            continue
                        h.wait_ge(sem, v)
                        known[key] = v
                    ins = o.fn(h) if o.fn is not None else None
                    if ins is None:
                        assert not o.signal and o.dsem is None
                        continue
                    if o.dsem is not None:
                        ins.then_inc(o.dsem.h, 16)
                    elif o.signal:
                        ins.then_inc(esem[e], 1)

            for e in self.ENGS:
                if not per[e]:
                    continue
                deco = getattr(block, self.BLK[e])

                def body(h, e=e):
                    run(e, h)

                deco(body)


def _rope_tables():
    t = np.arange(LS)
    row = (t // 64).astype(np.float32)
    col = (t % 64).astype(np.float32)
    inv = (10000.0 ** (-np.arange(0, 64, 2, dtype=np.float32) / 64.0)).astype(np.float32)
    ar = row[:, None] * inv
    ac = col[:, None] * inv
    ang = np.concatenate([ar, ar, ac, ac], axis=-1)
    out = np.zeros((128, 2, LS), np.float32)
    out[:, 0, :] = np.cos(ang).T
    out[:, 1, :] = np.sin(ang).T
    return out


def _pool_matrix(T, lo_t, n, w):
    P = np.zeros((n, n), np.float32)
    for o in range(n):
        t = lo_t + o
        lo = min(max(t - w // 2, 0), T)
        hi = min(max(t + w - w // 2, 0), T)
        cnt = float(hi - lo)
        for s in range(lo, hi):
            i = s - lo_t
            if 0 <= i < n:
                P[i, o] += 1.0 / cnt
        P[o, o] -= 1.0
    return P


def _const_b16():
    c = np.zeros((128, 5, 128), np.float32)
    c[:, 0, :] = np.eye(128)
    c[:, 1, :] = 1.0
    s = np.arange(128)[:, None]
    t = np.arange(128)[None, :]
    same = (s // CH) == (t // CH)
    c[:, 2, :] = -(same & (s <= t)).astype(np.float32)
    c[:, 3, :] = -(same & (s >= t)).astype(np.float32)
    R = np.zeros((128, 128), np.float32)
    for j in range(32):
        R[32 + j, j] = -1.0
        R[j, 32 + j] = 1.0
        R[96 + j, 64 + j] = -1.0
        R[64 + j, 96 + j] = 1.0
    c[:, 4, :] = R
    return c.reshape(128, 640)


def _const_f32():
    c = np.zeros((128, 256 + LS), np.float32)
    c[:, 0:128] = np.eye(128)
    c[:, 128:256] = 1.0
    m = np.ones(LS, np.float32)
    m[::CH] = 0.0
    c[:, 256:] = m[None, :]
    return c


def _ptab(q):
    out = np.zeros((128, 4, 2 * 256 + 3 * SEG), np.float32)
    for g, w in enumerate((2, 4, 8, 16)):
        Pp = _pool_matrix(256, 0, 256, w)
        for ti in range(2):
            out[:, g, ti * 256:(ti + 1) * 256] = Pp[ti * 128:(ti + 1) * 128, :]
        Ps = _pool_matrix(LS, SEG_STARTS[q], SEG, w)
        for ti in range(3):
            rows = Ps[ti * 128:min((ti + 1) * 128, SEG), :]
            out[:rows.shape[0], g, 512 + ti * SEG:512 + (ti + 1) * SEG] = rows
    return out.reshape(128, -1)


def _blockify(W, col0, ncols=512):
    sub = W[:, col0:col0 + ncols]
    return np.ascontiguousarray(sub.reshape(16, 128, ncols).transpose(1, 0, 2)).reshape(128, 16 * ncols)


def _build_weight_blocks(w_ada, w_in_ab, w_out_ab, w_pool, w_mlp_in, w_mlp_out):
    wb = np.empty((NWB, 128, 8192), np.float32)
    for l in range(2):
        for nb in range(NB_ADA):
            wb[WB_ADA0 + l * NB_ADA + nb] = _blockify(w_ada[l], nb * 512)
    Wi = w_in_ab[0]
    wb[WB_IN + 0] = _blockify(Wi, 0)
    wb[WB_IN + 1] = _blockify(Wi, 512)
    wb[WB_IN + 2] = _blockify(Wi, 1024)
    wb[WB_IN + 3] = _blockify(Wi, 5632)
    wb[WB_IN + 4] = _blockify(Wi, 5632 + 512)
    for h in range(8):
        cols = np.concatenate([np.arange(base + h * 128, base + (h + 1) * 128)
                               for base in (1536, 2560, 3584, 4608)])
        sub = Wi[:, cols]
        wb[WB_IN + 5 + h] = _blockify(sub, 0)
    for cb in range(4):
        wb[WB_OUT + cb] = _blockify(w_out_ab[0], cb * 512)
    for l, base in ((0, WB_MLP0), (1, WB_MLP1)):
        for s in range(4):
            for j in range(4):
                wb[base + s * 8 + j] = _blockify(w_mlp_in[l], s * 2048 + j * 512)
            W2s = w_mlp_out[l][s * 2048:(s + 1) * 2048, :]
            for cb in range(4):
                wb[base + s * 8 + 4 + cb] = _blockify(W2s, cb * 512)
    pw = np.zeros((2048, 512), np.float32)
    for g in range(4):
        pw[g * 512:(g + 1) * 512, :] = w_pool[0, g]
    wb[WB_POOL] = _blockify(pw, 0)
    return wb


def _build_vecs(core, c, c_ctx, b_ada, norm_mix, norm_mlp, q_norm, k_norm, hg_norm, lb_raw,
                pool_scale, final_norm):
    b = core // 4
    q = core % 4
    v = np.zeros((128, NV), np.float32)
    fm = lambda x: np.asarray(x, np.float32).reshape(-1, 128).T
    for l in range(2):
        v[:, V_NMIX + l * 16:V_NMIX + (l + 1) * 16] = fm(norm_mix[l])
        v[:, V_NMLP + l * 16:V_NMLP + (l + 1) * 16] = fm(norm_mlp[l])
        v[:, V_BADA + l * 96:V_BADA + (l + 1) * 96] = fm(b_ada[l])
    v[:, V_FIN:V_FIN + 16] = fm(final_norm)
    v[:, V_PSC:V_PSC + 16] = fm(pool_scale[0])
    v[:, V_QN] = q_norm[0]
    v[:, V_KN] = k_norm[0]
    v[:, V_HN] = hg_norm[0]
    for d in range(2):
        for j in range(3):
            v[:, V_LB + d * 24 + j * 8:V_LB + d * 24 + (j + 1) * 8] = fm(lb_raw[d, j])
    cv = np.stack([fm(c_ctx), fm(c[b])], axis=-1)
    v[:, V_CV:V_CV + 32] = cv.reshape(128, 32)
    v[:, V_OH + q] = 1.0
    return v


class Builder:
    def __init__(self, stage=99, dbg=(), wsched=None):
        self.wsched_in = wsched
        self.stage = stage
        self.dbg = dbg
        self.nc = bass.Bass("TRN2", target_bir_lowering=False)
        self.P = Prog(self.nc)
        self.st = ExitStack()
        self.bank_rr = 0
        self.dbg_out = []

    def dram_in(self, name, shape):
        return self.nc.dram_tensor(name, list(shape), F32, kind="ExternalInput").ap()

    def dram_out(self, name, shape):
        return self.nc.dram_tensor(name, list(shape), F32, kind="ExternalOutput").ap()

    def sb(self, name, shape, dt=F32):
        return self.st.enter_context(self.nc.sbuf_tensor("s_" + name, list(shape), dt))

    def mm(self, out, lhsT, rhs, start, stop, r, w):
        return self.P.add("pe", lambda h: h.matmul(out, lhsT, rhs, start=start, stop=stop), r, w)

    def tr(self, out, in_, ident, r, w):
        return self.P.add("pe", lambda h: h.transpose(out, in_, ident), r, w)

    def act(self, out, in_, func, r, w, bias=None, scale=None):
        kw = {}
        if bias is not None:
            kw["bias"] = bias
        if scale is not None:
            kw["scale"] = scale
        return self.P.add("act", lambda h: h.activation(out, in_, func, **kw), r, w)

    def tt(self, out, in0, in1, op, r, w, eng="dve"):
        return self.P.add(eng, lambda h: h.tensor_tensor(out, in0, in1, op), r, w)

    def ts(self, out, in0, s1, s2, op0, op1, r, w, eng="dve"):
        if op1 is None:
            return self.P.add(eng, lambda h: h.tensor_scalar(out, in0, s1, None, op0), r, w)
        return self.P.add(eng, lambda h: h.tensor_scalar(out, in0, s1, s2, op0, op1), r, w)

    def stt(self, out, in0, sc, in1, op0, op1, r, w, eng="dve"):
        return self.P.add(eng, lambda h: h.scalar_tensor_tensor(out, in0, sc, in1, op0, op1), r, w)

    def cp(self, out, in_, r, w, eng="dve"):
        return self.P.add(eng, lambda h: h.tensor_copy(out, in_), r, w)

    def dma(self, eng, out, in_, r, w, dsem):
        return self.P.add(eng, lambda h: h.dma_start(out=out, in_=in_), r, w, dsem=dsem)

    def bank(self):
        i = self.bank_rr
        self.bank_rr = (self.bank_rr + 1) % 6
        return self.pb[i], self.pbr[i]


    def rsqrt(self, bk, c, n, bkr):
        if not hasattr(self, "_cbias"):
            self._cbias = {}
        self.act(self.rstd[:, 0:n], bk[:, 0:n], AF.Ln, [bkr, self.r_mod], [self.r_rstd], bias=self.cconst(c))
        self.act(self.rstd[:, 0:n], self.rstd[:, 0:n], AF.Exp, [self.r_rstd], [self.r_rstd], scale=-0.5)

    def cconst(self, c):
        key = float(c)
        if key not in self._cc:
            idx = len(self._cc)
            ap = self.cct[:, idx:idx + 1]
            self.P.add("dve", lambda h, ap=ap, key=key: h.memset(ap, key), [], [self.r_mod])
            self._cc[key] = ap
        return self._cc[key]

    def wnext(self, blk):
        i = self.wpos
        self.wpos += 1
        if self.wsched is None:
            self.req.append(blk)
            self._wissue(i, blk)
        else:
            assert self.wsched[i] == blk, (i, blk, self.wsched[i])
            while self.wissued < min(len(self.wsched), i + 2):
                self._wissue(self.wissued, self.wsched[self.wissued])
                self.wissued += 1
        s = i % 2
        return self.wring[s], self.wres[s]

    def _wissue(self, i, blk):
        s = i % 2
        dst = self.wring[s]
        self.dma("pool", dst[:, :, :].rearrange("p a b -> p (a b)"), self.wb[blk, :, :], [],
                 [self.wres[s]], self.wsem[s])

    def build(self):
        nc, P = self.nc, self.P
        self.xm = self.dram_in("xm", [NM, D])
        self.xs = self.dram_in("xs", [LS, D])
        self.ck = self.dram_in("ck", [256, 256])
        self.cv = self.dram_in("cv", [256, 256])
        self.s0f = self.dram_in("s0f", [8, 128, 128])
        self.s0b = self.dram_in("s0b", [8, 128, 128])
        self.vecs_d = self.dram_in("vecs", [128, NV])
        self.cf32_d = self.dram_in("cf32", [128, 256 + LS])
        self.cb16_d = self.dram_in("cb16", [128, 640])
        self.rope_d = self.dram_in("rope", [128, 2 * LS])
        self.ropeseg_d = self.dram_in("ropeseg", [128, 2 * SEG])
        self.finrep_d = self.dram_in("finrep", [128, D])
        self.ptab_d = self.dram_in("ptab", [128, 4 * (512 + 3 * SEG)])
        self.wb = self.dram_in("wb", [NWB, 128, 8192])
        self.y = self.dram_out("y", [NM, D])
        self.nk = self.dram_out("nk", [NP, 256])
        self.nv = self.dram_out("nv", [NP, 256])
        self.nsf = self.dram_out("nsf", [2, 8, 128, 128])
        self.nsb = self.dram_out("nsb", [2, 8, 128, 128])
        self.out_res = [Res("o_y"), Res("o_nk"), Res("o_nv"), Res("o_nsf"), Res("o_nsb")]

        self.xTp = self.sb("xTp", [128, NCH, NP])
        self.xTs_flat = self.sb("xTs", [128, NCH * SEG])
        self.xTs = self.xTs_flat[:, :].rearrange("p (a b) -> p a b", b=SEG)
        self.hT = self.sb("hT", [128, NCH, LS], BF16)
        self.mixT = self.sb("mixT", [128, NCH, NP], BF16)
        self.wring = [self.sb("wr%d" % i, [128, 16, 512], BF16) for i in range(2)]
        self.vecs = self.sb("vecs", [128, NV])
        self.cf32 = self.sb("cf32", [128, 256 + LS])
        self.cb16 = self.sb("cb16", [128, 5, 128], BF16)
        self.mod = self.sb("mod", [128, 2, 96, 2])
        self.modx = self.sb("modx", [128, 2, 2, 7, 16])
        self.lbt = self.sb("lbt", [128, 2, 4, 8])
        self.gains = self.sb("gains", [128, 4])
        self.cct = self.sb("cct", [128, 8])
        self._cc = {}
        self.sqb = [self.sb("sqb%d" % i, [128, 512]) for i in range(2)]
        self.sqh = [t[:, :].bitcast(BF16) for t in self.sqb]
        self.rstd = self.sb("rstd", [128, 512])
        self.tmpf = [self.sb("tmpf%d" % i, [128, 512]) for i in range(2)]
        self.arena = self.sb("arena", [128, 14080])
        self.pb = [self.st.enter_context(nc.psum_tensor("pb%d" % i, [128, 512], F32)) for i in range(8)]
        self.pbr = [Res("pb%d" % i) for i in range(8)]

        self.ident_f = self.cf32[:, 0:128]
        self.ones_f = self.cf32[:, 128:256]
        self.scanmask = self.cf32[:, 256:256 + LS]
        self.ident_b = self.cb16[:, 0, :]
        self.ones_b = self.cb16[:, 1, :]
        self.nmask = [self.cb16[:, 2, :], self.cb16[:, 3, :]]
        self.rotm = self.cb16[:, 4, :]

        self.r_const = Res("const")
        self.r_mod = Res("mod")
        self.r_xTp = [Res("xTp%d" % c) for c in range(NCH)]
        self.r_xTs = [Res("xTs%d" % c) for c in range(NCH)]
        self.r_hT = Res("hT")
        self.r_mix = [Res("mix%d" % c) for c in range(NCH)]
        self.wres = [Res("w0"), Res("w1")]
        self.wsem = [P.dsem("w0", "pool"), P.dsem("w1", "pool")]
        self.r_sqb = [Res("sqb0"), Res("sqb1")]
        self.r_sq4 = [Res("sq4_%d" % i) for i in range(4)]
        self.r_rstd = Res("rstd")
        self.r_tmpf = [Res("tmpf0"), Res("tmpf1")]
        self.r_ar = {}

        self.wsched = self.wsched_in
        self.req = []
        self.wpos = 0
        self.wissued = 0

        csem = P.dsem("c_sp", "sp")
        self.dma("sp", self.vecs[:, :], self.vecs_d, [], [self.r_const], csem)
        self.dma("sp", self.cf32[:, :], self.cf32_d, [], [self.r_const], csem)
        csem2 = P.dsem("c_pool", "pool")
        self.dma("pool", self.cb16[:, :, :].rearrange("p a b -> p (a b)"), self.cb16_d, [],
                 [self.r_const], csem2)

        self.phase_adaln()
        if self.stage >= 1:
            self.phase_load_main()
        if self.stage >= 2:
            self.phase_mixer(sample=False)
        if self.stage >= 3:
            self.phase_mixer(sample=True)
        if self.stage >= 4:
            self.phase_mlp(0)
        if self.stage >= 5:
            self.phase_pool()
            self.phase_mlp(1)
        if self.stage >= 6:
            self.phase_final()
        self.finish()
        return nc

    def ar(self, name, off, words, shape, dt=F32):
        v = self.arena[:, off:off + words]
        if dt == BF16:
            v = v.bitcast(BF16)
        if len(shape) == 3:
            v = v.rearrange("p (a b) -> p a b", b=shape[2])
        elif len(shape) == 4:
            v = v.rearrange("p (a b c) -> p a b c", b=shape[2], c=shape[3])
        assert tuple(v.shape) == tuple(shape), (name, v.shape, shape)
        if name not in self.r_ar:
            self.r_ar[name] = Res(name)
        return v, self.r_ar[name]

    def phase_adaln(self):
        P = self.P
        V = self.vecs
        rc = [self.r_const]
        self.ada_sg = self.sb("ada_sg", [128, 32])[:, :]
        self.ada_sT = self.sb("ada_sT", [128, 32], BF16)[:, :]
        self.r_sT = Res("ada_sT")
        r_sg = Res("ada_sg")
        self.act(self.ada_sg, V[:, V_CV:V_CV + 32], AF.Sigmoid, rc, [r_sg])
        self.tt(self.ada_sT, V[:, V_CV:V_CV + 32], self.ada_sg, ALU.mult, rc + [r_sg], [self.r_sT])
        for nb in range(8):
            self.adaln_block(0, nb)
        self.adaln_finish(0, part=0)

    def adaln_block(self, l, nb):
        self.adaln_tr(l, nb, self.adaln_mm(l, nb, 0))

    def adaln_mm(self, l, nb, bi):
        sT = self.ada_sT
        w, wr = self.wnext(WB_ADA0 + l * NB_ADA + nb)
        bk, bkr = self.bank()
        for kc in range(16):
            self.mm(bk[0:2, :], sT[:, 2 * kc:2 * kc + 2], w[:, kc, :], kc == 0, kc == 15,
                    [wr, self.r_sT], [bkr])
        return (bk, bkr, bi)

    def adaln_tr(self, l, nb, st):
        bk, bkr, bi = st
        modps, r_modps = self.pb[7], self.pbr[7]
        mt, mtr = [(self.sqb[1], self.r_sqb[1]), (self.tmpf[1], self.r_tmpf[1]), (self.rstd, self.r_rstd)][bi]
        self.act(mt[0:2, :], bk[0:2, :], AF.Copy, [bkr], [mtr])
        for j in range(4):
            ch = nb * 4 + j
            self.mm(modps[:, 2 * ch:2 * ch + 2], mt[0:2, j * 128:(j + 1) * 128],
                    self.ident_f[0:2, 0:2], True, True, [mtr, self.r_const], [r_modps])

    def adaln_finish(self, l, part=None):
        V = self.vecs
        rc = [self.r_const]
        modps, r_modps = self.pb[7], self.pbr[7]
        c0, c1 = {None: (0, 96), 0: (0, 32), 1: (32, 96)}[part]
        bada = V[:, V_BADA + l * 96 + c0:V_BADA + l * 96 + c1].unsqueeze(2).broadcast_to([128, c1 - c0, 2])
        self.tt(self.mod[:, l, c0:c1, :], modps[:, 2 * c0:2 * c1].rearrange("p (a b) -> p a b", b=2), bada,
                ALU.add, [r_modps] + rc, [self.r_mod])
        sD = float(np.sqrt(D))
        rm = [self.r_mod]
        for v in range(2):
            M = lambda j: self.mod[:, l, j * 16:(j + 1) * 16, v]
            X = lambda k: self.modx[:, l, v, k, :]
            if part in (None, 0):
                self.stt(X(0), M(1), 1.0, V[:, V_NMIX + l * 16:V_NMIX + (l + 1) * 16], ALU.add, ALU.mult,
                         rm + rc, [self.r_mod])
                self.ts(X(0), X(0), sD, None, ALU.mult, None, rm, [self.r_mod])
                self.cp(X(1), M(0), rm, [self.r_mod])
            if part in (None, 1):
                self.cp(X(2), M(2), rm, [self.r_mod])
                self.stt(X(3), M(4), 1.0, V[:, V_NMLP + l * 16:V_NMLP + (l + 1) * 16], ALU.add, ALU.mult,
                         rm + rc, [self.r_mod])
                self.ts(X(3), X(3), sD, None, ALU.mult, None, rm, [self.r_mod])
                self.cp(X(4), M(3), rm, [self.r_mod])
                self.cp(X(5), M(5), rm, [self.r_mod])
                self.tt(X(6), M(2), V[:, V_PSC:V_PSC + 16], ALU.mult, rm + rc, [self.r_mod])
        if l == 1 or part == 1:
            return
        s128 = float(np.sqrt(128.0))
        self.ts(self.gains[:, 0:3], V[:, V_QN:V_QN + 3], s128, None, ALU.mult, None, rc, [self.r_mod])
        for d in range(2):
            e3, r_e3 = self.ar("lb_e", 1100, 24, [128, 24])
            self.act(e3, V[:, V_LB + d * 24:V_LB + (d + 1) * 24], AF.Exp, rc, [r_e3])
            tmp = self.lbt[:, d, 3, :]
            self.tt(tmp, e3[:, 0:8], e3[:, 8:16], ALU.add, [r_e3], [self.r_mod])
            self.tt(tmp, tmp, e3[:, 16:24], ALU.add, [r_e3, self.r_mod], [self.r_mod])
            self.P.add("dve", lambda h, tmp=tmp: h.reciprocal(tmp, tmp), [self.r_mod], [self.r_mod])
            self.tt(self.lbt[:, d, 0, :], e3[:, 0:8], tmp, ALU.mult, [r_e3, self.r_mod], [self.r_mod])
            self.ts(self.lbt[:, d, 1, :], self.lbt[:, d, 0, :], -1.0, 1.0, ALU.mult, ALU.add,
                    [self.r_mod], [self.r_mod])
            self.act(self.lbt[:, d, 2, :], self.lbt[:, d, 1, :], AF.Ln, [self.r_mod], [self.r_mod])
        if "mod" in self.dbg:
            self.dump("d_mod", self.mod[:, :, :, :].rearrange("p a b c -> p (a b c)"), [self.r_mod], 384)
            self.dump("d_lbt", self.lbt[:, :, :, :].rearrange("p a b c -> p (a b c)"), [self.r_mod], 64)

    def dump(self, name, ap, reads, ncols, dt=F32):
        d = self.nc.dram_tensor(name, [128, ncols], dt, kind="ExternalOutput").ap()
        if len(ap.shape) == 3:
            d = d.rearrange("p (a b) -> p a b", b=ap.shape[2])
        r = Res("dbg_" + name)
        ds = self.P.dsem("dbg_" + name, "sp")
        self.dma("sp", d, ap, reads, [r], ds)
        self.out_res.append(r)
        self.dbg_out.append(name)

    def finish(self):
        self.P.pending["sp"] = [d.last for d in self.P.dsems if d.last is not None]
        self.P.add("sp", None, [], [])
        if self.wsched is not None:
            self.P.emit()
        self.st.close()


    def load_tokens(self, src, rows, dst_fn, xin, xsem, t_off=0):
        for ti, (r0, n) in enumerate(rows):
            xi, xr = xin[ti % 2]
            self.dma("sp", xi[0:n, :], src[r0:r0 + n, :], [], [xr], xsem[ti % 2])
            for cg in range(4):
                bk, bkr = self.bank()
                for j in range(4):
                    c = cg * 4 + j
                    self.tr(bk[0:128, j * 128:j * 128 + n], xi[0:n, c * 128:(c + 1) * 128],
                            self.ident_f[0:n, 0:n], [xr, self.r_const], [bkr])
                dst, dres = dst_fn(cg, r0 - t_off, n)
                srcv = bk[:, :].rearrange("p (a b) -> p a b", b=128)[:, :, 0:n]
                self.act(dst, srcv, AF.Copy, [bkr], dres)

    def xin_bufs(self):
        if not hasattr(self, "_xsem"):
            self._xsem = [self.P.dsem("xin0", "sp"), self.P.dsem("xin1", "sp")]
        xin = [self.ar("xin%d" % i, i * 2048, 2048, [128, 2048]) for i in range(2)]
        return xin, self._xsem

    def phase_load_main(self):
        self.P.barrier()
        xin, xsem = self.xin_bufs()
        rows = [(r0, 128) for r0 in range(0, NP, 128)]
        self.load_tokens(self.xm, rows,
                         lambda cg, t0, n: (self.xTp[:, cg * 4:(cg + 1) * 4, t0:t0 + n],
                                            self.r_xTp[cg * 4:(cg + 1) * 4]), xin, xsem)
        if "xT" in self.dbg:
            self.dump("d_xTp", self.xTp[:, :, :].rearrange("p a b -> p (a b)"), self.r_xTp, NCH * NP)

    def load_seg(self):
        xin, xsem = self.xin_bufs()
        rows = [(NP, 128), (NP + 128, 128), (NP + 256, 16)]
        self.load_tokens(self.xm, rows,
                         lambda cg, t0, n: (self.xTs[:, cg * 4:(cg + 1) * 4, t0:t0 + n],
                                            self.r_xTs[cg * 4:(cg + 1) * 4]), xin, xsem, t_off=NP)

    def norm_mod(self, xsrc, ranges, l, k0):
        sq4 = [(self.sqh[i // 2][:, (i % 2) * 512:(i % 2) * 512 + 512], self.r_sq4[i]) for i in range(4)]
        t4 = [(self.tmpf[0], [self.r_tmpf[0]]), (self.tmpf[1], [self.r_tmpf[1]]),
              (self.sqb[0], [self.r_sq4[0], self.r_sq4[1]]), (self.sqb[1], [self.r_sq4[2], self.r_sq4[3]])]
        for (t0, n, v, h0) in ranges:
            bk, bkr = self.bank()
            for c in range(NCH):
                xa, xr = xsrc(c, t0, n)
                sq, sqr = sq4[c % 4]
                if c % 2 == 0:
                    self.act(sq[:, 0:n], xa, AF.Square, [xr], [sqr])
                else:
                    self.tt(sq[:, 0:n], xa, xa, ALU.mult, [xr], [sqr])
                self.mm(bk[:, 0:n], self.ones_b, sq[:, 0:n], c == 0, c == NCH - 1, [sqr, self.r_const], [bkr])
            self.rsqrt(bk, D * EPS, n, bkr)
            for c in range(NCH):
                xa, xr = xsrc(c, t0, n)
                tf, tfr = t4[c % 4]
                self.stt(tf[:, 0:n], xa, self.modx[:, l, v, k0, c:c + 1], self.rstd[:, 0:n], ALU.mult, ALU.mult,
                         [xr, self.r_mod, self.r_rstd], tfr)
                self.act(self.hT[:, c, h0:h0 + n], tf[:, 0:n], AF.Identity, tfr + [self.r_mod], [self.r_hT],
                         bias=self.modx[:, l, v, k0 + 1, c:c + 1])

    def xmain(self, c, t0, n):
        if t0 < NP:
            assert t0 + n <= NP
            return self.xTp[:, c, t0:t0 + n], self.r_xTp[c]
        return self.xTs[:, c, t0 - NP:t0 - NP + n], self.r_xTs[c]

    def proj(self, w, wr, ms, tiles, rhs, rhs_res, evac, nk=16, kc0=0):
        for m in ms:
            for (t0, n) in tiles:
                bk, bkr = self.bank()
                for kc in range(nk):
                    self.mm(bk[:, 0:n], w[:, kc0 + kc, m * 128:(m + 1) * 128], rhs(kc, t0, n), kc == 0,
                            kc == nk - 1, [wr] + rhs_res, [bkr])
                evac(m, t0, n, bk, bkr)

    def proj_units(self, wfn, ms, tiles, rhs, rhs_res, evac):
        state = {}

        def unit(m, t0, n):
            if "w" not in state:
                state["w"] = wfn()
            w, wr = state["w"]
            bk, bkr = self.bank()
            for kc in range(16):
                self.mm(bk[:, 0:n], w[:, kc, m * 128:(m + 1) * 128], rhs(kc, t0, n), kc == 0, kc == 15,
                        [wr] + rhs_res, [bkr])
            evac(m, t0, n, bk, bkr)
        return [(lambda m=m, t0=t0, n=n: unit(m, t0, n)) for m in ms for (t0, n) in tiles]

    def headnorm(self, bk, bkr, n, gain_col, dst, dres, rope_cols=None, want_f32=False):
        qf, qfr = self.tmpf[0], self.r_tmpf[0]
        sq, sqr = self.sqb[0], self.r_sqb[0]
        g = self.gains[:, gain_col:gain_col + 1]
        self.act(qf[:, 0:n], bk[:, 0:n], AF.Copy, [bkr, self.r_mod], [qfr], scale=g)
        self.act(self.sqh[0][:, 0:n], bk[:, 0:n], AF.Square, [bkr], [sqr])
        b2, b2r = self.bank()
        self.mm(b2[:, 0:n], self.ones_b, self.sqh[0][:, 0:n], True, True, [sqr, self.r_const], [b2r])
        self.rsqrt(b2, 128.0 * EPS, n, b2r)
        if rope_cols is None and not want_f32:
            self.tt(dst, qf[:, 0:n], self.rstd[:, 0:n], ALU.mult, [qfr, self.r_rstd], dres)
            return None, None
        qn, qnr = self.tmpf[1], self.r_tmpf[1]
        self.tt(qn[:, 0:n], qf[:, 0:n], self.rstd[:, 0:n], ALU.mult, [qfr, self.r_rstd], [qnr])
        if rope_cols is None:
            self.act(dst, qn[:, 0:n], AF.Copy, [qnr], dres)
            return qn, qnr
        cos, sin, rr = rope_cols
        qnb, qnbr = self.ar("qnb", 8704, 256, [128, 512], BF16)
        self.act(qnb[:, 0:n], qn[:, 0:n], AF.Copy, [qnr], [qnbr])
        b3, b3r = self.bank()
        self.mm(b3[:, 0:n], self.rotm, qnb[:, 0:n], True, True, [qnbr, self.r_const], [b3r])
        t1, t1r = self.ar("ropet1", 8960, 512, [128, 512])
        self.tt(t1[:, 0:n], qn[:, 0:n], cos, ALU.mult, [qnr, rr], [t1r])
        self.tt(sq[:, 0:n], b3[:, 0:n], sin, ALU.mult, [b3r, rr], [sqr])
        self.tt(dst, t1[:, 0:n], sq[:, 0:n], ALU.add, [t1r, sqr], dres)
        return None, None

    def extract(self, dst, dres, src, sres):
        oh = lambda q: self.vecs[:, V_OH + q:V_OH + q + 1]
        self.ts(dst, src[:, 0:SEG], oh(0), None, ALU.mult, None, sres + [self.r_const], dres)
        for q in range(1, 4):
            s0 = SEG_STARTS[q]
            self.stt(dst, src[:, s0:s0 + SEG], oh(q), dst, ALU.mult, ALU.add, sres + dres + [self.r_const], dres)

    def phase_mixer(self, sample):
        P = self.P
        Lg = LS if sample else NP
        v = 1 if sample else 0
        seqs = [(0, LS)] if sample else [(0, 256), (256, 256)]
        tiles = [(t0, 512) for t0 in range(0, Lg, 512)]
        ntile = Lg // 128
        P.barrier()
        if not sample:
            self.norm_mod(lambda c, t0, n: (self.xTp[:, c, t0:t0 + n], self.r_xTp[c]), [(0, 512, 0, 0)], 0, 0)
        else:
            xin, xsem = self.xin_bufs()
            xsTs = [self.ar("xsT%d" % i, 4096 + 4096 * i, 4096, [128, 16, 256]) for i in range(2)]
            for blk in range(4):
                xsT, xsTr = xsTs[blk % 2]
                self.load_tokens(self.xs, [(blk * 256, 128), (blk * 256 + 128, 128)],
                                 lambda cg, t0, n, xsT=xsT, xsTr=xsTr: (xsT[:, cg * 4:(cg + 1) * 4, t0:t0 + n], [xsTr]),
                                 xin, xsem, t_off=blk * 256)
                if blk >= 1:
                    pT_, pTr_ = xsTs[(blk - 1) % 2]
                    self.norm_mod(lambda c, t0, n, pT_=pT_, pTr_=pTr_: (pT_[:, c, t0:t0 + n], pTr_),
                                  [(0, 256, 1, (blk - 1) * 256)], 0, 0)
            pT_, pTr_ = xsTs[3 % 2]
            self.norm_mod(lambda c, t0, n, pT_=pT_, pTr_=pTr_: (pT_[:, c, t0:t0 + n], pTr_), [(0, 256, 1, 3 * 256)], 0, 0)
        if "hT" in self.dbg:
            self.dump("d_hT%d" % v, self.hT[:, :, 0:Lg], [self.r_hT], NCH * Lg, BF16)
        if sample:
            hseg, hsegr = self.xsv("hseg", 0, 2176, [128, 16, SEG], BF16)
            for c in range(NCH):
                self.extract(hseg[:, c, :], [hsegr], self.hT[:, c, :], [self.r_hT])
        sub = int(os.environ.get("KSUB", "9"))
        if sub < 1:
            return
        P.barrier()
        hrhs = lambda kc, t0, n: self.hT[:, kc, t0:t0 + n]
        hres = [self.r_hT]
        qT, qTr = self.ar("qT", 0, 4096, [128, 8, LS], BF16)
        kT, kTr = self.ar("kT", 4096, 1280, [128, 2, 1280], BF16)
        vtok, vtokr = self.ar("vtok", 5376, 1280, [128, 10, 256], BF16)
        pT = [self.ar("pT%d" % i, 6656 + 256 * i, 256, [128, 512], BF16) for i in range(2)]
        rden, rdenr = self.ar("rden", 12032, 512, [128, 512])
        if sample:
            rope, roper = self.ar("rope", 9472, 2048, [128, 2, LS])
            ropes, ropesr = self.ar("ropeseg", 7168, 2 * SEG, [128, 2, SEG])
            ckst, ckstr = self.ar("ckst", 12544, 512, [128, 2, 256])
            cvst, cvstr = self.ar("cvst", 13056, 512, [128, 2, 256])
            rsem = P.dsem("rope", "sp")
            self.dma("sp", rope[:, :, :].rearrange("p a b -> p (a b)"), self.rope_d, [], [roper], rsem)
            self.dma("sp", ropes[:, :, :].rearrange("p a b -> p (a b)"), self.ropeseg_d, [], [ropesr], rsem)
            self.dma("sp", ckst, self.ck.rearrange("(t p) f -> p t f", p=128), [], [ckstr], rsem)
            self.dma("sp", cvst, self.cv.rearrange("(t p) f -> p t f", p=128), [], [cvstr], rsem)
        else:
            nkst, nkstr = self.ar("nkst", 9472, 1024, [128, 4, 256])
            nvst, nvstr = self.ar("nvst", 10496, 1024, [128, 4, 256])
            osem = P.dsem("nkv", "sp")

        for blk in range(2):
            w, wr = self.wnext(WB_IN + blk)

            def evq(m, t0, n, bk, bkr, blk=blk):
                hq = blk * 4 + m
                rc = (ropes[:, 0, t0:t0 + n], ropes[:, 1, t0:t0 + n], ropesr) if sample else None
                self.headnorm(bk, bkr, n, 0, qT[:, hq, t0:t0 + n], [qTr], rc)
            if sample:
                self.proj(w, wr, range(4), [(0, SEG)], lambda kc, t0, n: hseg[:, kc, t0:t0 + n], [hsegr], evq)
            else:
                self.proj(w, wr, range(4), tiles, hrhs, hres, evq)
        kss = int(os.environ.get("KSS", "9"))
        if kss < 1:
            return
        w, wr = self.wnext(WB_IN + 2)

        def evk(m, t0, n, bk, bkr):
            rc = (rope[:, 0, t0:t0 + n], rope[:, 1, t0:t0 + n], roper) if sample else None
            qn, qnr = self.headnorm(bk, bkr, n, 1, kT[:, m, t0:t0 + n], [kTr], rc, want_f32=not sample)
            if not sample:
                for tt in range(n // 128):
                    b4, b4r = self.bank()
                    self.tr(b4[:, 0:128], qn[:, tt * 128:(tt + 1) * 128], self.ident_f, [qnr, self.r_const], [b4r])
                    self.act(nkst[:, (t0 // 128) + tt, m * 128:(m + 1) * 128], b4[:, 0:128], AF.Copy, [b4r], [nkstr])
        self.proj(w, wr, range(2), tiles, hrhs, hres, evk)
        if kss < 2:
            return
        for tt in range(ntile):
            bk, bkr = self.bank()
            for kc in range(16):
                self.mm(bk[:, 0:256], self.hT[:, kc, tt * 128:(tt + 1) * 128], w[:, kc, 256:512], kc == 0, kc == 15,
                        [wr, self.r_hT], [bkr])
            if sample:
                self.act(vtok[:, tt, :], bk[:, 0:256], AF.Copy, [bkr], [vtokr])
            else:
                self.act(nvst[:, tt, :], bk[:, 0:256], AF.Copy, [bkr], [nvstr])
                self.cp(vtok[:, tt, :], nvst[:, tt, :], [nvstr], [vtokr])
        if kss < 3:
            return
        if not sample:
            self.dma("sp", self.nk.rearrange("(t p) f -> p t f", p=128), nkst, [nkstr], [self.out_res[1]], osem)
            self.dma("sp", self.nv.rearrange("(t p) f -> p t f", p=128), nvst, [nvstr], [self.out_res[2]], osem)
        else:
            for tt in range(2):
                for j in range(2):
                    b4, b4r = self.bank()
                    self.tr(b4[:, 0:128], ckst[:, tt, j * 128:(j + 1) * 128], self.ident_f, [ckstr, self.r_const], [b4r])
                    self.act(kT[:, j, LS + tt * 128:LS + (tt + 1) * 128], b4[:, 0:128], AF.Copy, [b4r], [kTr])
                self.act(vtok[:, 8 + tt, :], cvst[:, tt, :], AF.Copy, [cvstr], [vtokr])
        if "qk" in self.dbg:
            self.dump("d_qT%d" % v, qT[:, :, 0:Lg], [qTr], 8 * Lg, BF16)
            self.dump("d_kT%d" % v, kT[:, :, :].rearrange("p a b -> p (a b)"), [kTr], 2560, BF16)
            self.dump("d_vtok%d" % v, vtok[:, :, :].rearrange("p a b -> p (a b)"), [vtokr], 2560, BF16)

        if sub < 2:
            return
        scale = float(128.0 ** -0.5)
        accs = [(3, 4), (5, 6)]
        acc_i = 0
        sb_i = 0
        for si, (s0, L) in enumerate(seqs):
            if sample:
                ktiles = list(range(10))
                qblocks = [(0, SEG)]
            else:
                ktiles = [si * 2, si * 2 + 1]
                qblocks = [(s0, 256)]
            nk_ = len(ktiles)
            for j in range(2):
                for g in range(4):
                    hq = j * 4 + g
                    for (t0, n) in qblocks:
                        oi, di = accs[acc_i % 2]
                        acc_i += 1
                        ob, obr, db, dbr = self.pb[oi], self.pbr[oi], self.pb[di], self.pbr[di]
                        sbk = [None] * nk_

                        def issue_s(ki):
                            nonlocal sb_i
                            bi = sb_i % 3
                            sb_i += 1
                            kt = ktiles[ki]
                            self.mm(self.pb[bi][:, 0:n], kT[:, j, kt * 128:kt * 128 + 128], qT[:, hq, t0:t0 + n], True, True,
                                    [kTr, qTr], [self.pbr[bi]])
                            sbk[ki] = bi
                        issue_s(0)
                        if nk_ > 1:
                            issue_s(1)
                        for ki, kt in enumerate(ktiles):
                            if ki + 2 < nk_:
                                issue_s(ki + 2)
                            bi = sbk[ki]
                            pt, ptr = pT[ki % 2]
                            self.act(pt[:, 0:n], self.pb[bi][:, 0:n], AF.Exp, [self.pbr[bi]], [ptr], scale=scale)
                            first, last = ki == 0, ki == nk_ - 1
                            self.mm(ob[:, 0:n], vtok[:, kt, j * 128:(j + 1) * 128], pt[:, 0:n], first, last,
                                    [vtokr, ptr], [obr])
                            self.mm(db[:, 0:n], self.ones_b, pt[:, 0:n], first, last, [ptr, self.r_const], [dbr])
                        self.P.add("dve", lambda h, n=n, db=db: h.reciprocal(rden[:, 0:n], db[:, 0:n]), [dbr], [rdenr])
                        self.tt(self.mixT[:, hq, t0:t0 + n], ob[:, 0:n], rden[:, 0:n], ALU.mult, [obr, rdenr],
                                [self.r_mix[hq]])
        if "att" in self.dbg:
            self.dump("d_att%d" % v, self.mixT[:, 0:8, :].rearrange("p a b -> p (a b)"), self.r_mix[0:8], 8 * NP, BF16)
        if sub < 3:
            return
        self.hgrn(sample, Lg, seqs, tiles, ntile, hrhs, hres)
        if sub < 4:
            return
        P.barrier()
        if sample:
            self.load_seg()
            otiles = [(0, SEG)]
        else:
            otiles = [(0, NP)]
        mrhs = lambda kc, t0, n: self.mixT[:, kc, t0:t0 + n]
        for cb in range(4):
            w, wr = self.wnext(WB_OUT + cb)

            def evo(m, t0, n, bk, bkr, cb=cb):
                c = cb * 4 + m
                if sample:
                    xa, xr = self.xTs[:, c, t0:t0 + n], self.r_xTs[c]
                else:
                    xa, xr = self.xTp[:, c, t0:t0 + n], self.r_xTp[c]
                self.stt(xa, bk[:, 0:n], self.modx[:, 0, v, 2, c:c + 1], xa, ALU.mult, ALU.add,
                         [bkr, xr, self.r_mod], [xr])
            self.proj(w, wr, range(4), otiles, mrhs, self.r_mix, evo)
        if "x1" in self.dbg:
            if sample:
                self.dump("d_x1s", self.xTs_flat[:, :], self.r_xTs, NCH * SEG)
            else:
                self.dump("d_x1p", self.xTp[:, :, :].rearrange("p a b -> p (a b)"), self.r_xTp, NCH * NP)

    def xsv(self, name, off, words, shape, dt=F32):
        v = self.xTs_flat[:, off:off + words]
        if dt == BF16:
            v = v.bitcast(BF16)
        if len(shape) == 3:
            v = v.rearrange("p (a b) -> p a b", b=shape[2])
        elif len(shape) == 4:
            v = v.rearrange("p (a b c) -> p a b c", b=shape[2], c=shape[3])
        assert tuple(v.shape) == tuple(shape), (name, v.shape, shape)
        if name not in self.r_ar:
            self.r_ar[name] = Res(name)
        return v, self.r_ar[name]

    def hgrn(self, sample, Lg, seqs, tiles, ntile, hrhs, hres):
        P = self.P
        P.barrier()
        v = 1 if sample else 0
        nch = Lg // CH
        gsT, gsTr = self.ar("gsT", 0, 2048, [128, 4, LS], BF16)
        qs, qsr = self.ar("qs", 2048, 1024, [128, LS])
        sgd = [self.ar("sgf", 3072, 1024, [128, LS]), self.ar("sgb", 4096, 1024, [128, LS])]
        hiT, hiTr = self.ar("hiT", 5120, 512, [128, LS], BF16)
        itok, itokr = self.ar("itok", 5632, 512, [128, 8, 128], BF16)
        A, Ar = self.ar("hA", 6144, 1024, [128, LS])
        Bb, Br = self.ar("hB", 7168, 1024, [128, LS])
        C, Cr = self.ar("hC", 8192, 1024, [128, LS])
        Qt = [self.ar("Qt%d" % d, 9216 + 512 * d, 512, [128, LS], BF16) for d in range(2)]
        Kt = [self.ar("Kt%d" % d, 10240 + 512 * d, 512, [128, LS], BF16) for d in range(2)]
        Ktok = [[self.ar("Ktok%d%d" % (d, i), 11264 + 64 * (2 * d + i), 64, [128, 128], BF16) for i in range(2)]
                for d in range(2)]
        Sp = [self.ar("Sp%d" % d, 11520 + 1024 * d, 1024, [128, 16, 128], BF16) for d in range(2)]
        attm = [self.xsv("attm%d" % d, 512 * d, 512, [128, 8, 128], BF16) for d in range(2)]
        Sb = [[self.xsv("S%d%d" % (d, i), 1024 + 128 * (2 * d + i), 128, [128, 128]) for i in range(2)]
              for d in range(2)]
        dtmp = [[self.xsv("dt%d%d" % (d, i), 1536 + 128 * (2 * d + i), 128, [128, 128]) for i in range(2)]
                for d in range(2)]
        oT, oTr = self.xsv("oT", 2048, 512, [128, 512])
        scal, scalr = self.xsv("scal", 2560, 160, [128, 2, 5, 16])
        sgq, sgqr = self.xsv("sgq", 2720, 512, [128, 512])
        otmp, otmpr = self.xsv("otmp", 3232, 512, [128, LS], BF16)
        if not hasattr(self, "_ssem"):
            self._ssem = [P.dsem("s0ld%d" % d, "sp") for d in range(2)]
            self._osem = [P.dsem("sout%d" % d, "sp") for d in range(2)]
        ob, obr = self.pb[6], self.pbr[6]

        itoks = [(itok, itokr), self.ar("itok2", 13568, 512, [128, 8, 128], BF16)]

        def emit_hg(half):
            w, wr = self.wnext(WB_IN + 3 + half)

            def evg(m, t0, n, bk, bkr):
                self.act(sgq[:, 0:n], bk[:, 0:n], AF.Sigmoid, [bkr], [sgqr])
                self.tt(gsT[:, m, t0:t0 + n], bk[:, 0:n], sgq[:, 0:n], ALU.mult, [bkr, sgqr], [gsTr])
            self.proj(w, wr, range(4), tiles, hrhs, hres, evg)

        def proj_head_units(h):
            def evh(m, t0, n, bk, bkr):
                if m == 0:
                    self.act(sgq[:, 0:n], bk[:, 0:n], AF.Sigmoid, [bkr], [sgqr])
                    self.tt(qs[:, t0:t0 + n], bk[:, 0:n], sgq[:, 0:n], ALU.mult, [bkr, sgqr], [qsr])
                elif m in (1, 2):
                    sg, sgr = sgd[m - 1]
                    self.act(sg[:, t0:t0 + n], bk[:, 0:n], AF.Sigmoid, [bkr], [sgr])
                else:
                    self.act(hiT[:, t0:t0 + n], bk[:, 0:n], AF.Copy, [bkr], [hiTr])
            units = self.proj_units(lambda: self.wnext(WB_IN + 5 + h), range(4), tiles, hrhs, hres, evh)

            def itok_unit():
                it, itr = itoks[h % 2]
                for tt in range(ntile):
                    bk, bkr = self.bank()
                    bkb = bk[:, 0:64].bitcast(BF16)
                    self.tr(bkb, hiT[:, tt * 128:(tt + 1) * 128], self.ident_b, [hiTr, self.r_const], [bkr])
                    self.act(it[:, tt, :], bkb, AF.Copy, [bkr], [itr])
            return units + [itok_unit]

        def emit_prep(h):
            rm = [self.r_mod]
            for d in range(2):
                sg, sgr = sgd[d]
                lb = self.lbt[:, d, 0, h:h + 1]
                oml = self.lbt[:, d, 1, h:h + 1]
                lnoml = self.lbt[:, d, 2, h:h + 1]
                self.act(A[:, 0:Lg], sg[:, 0:Lg], AF.Ln, [sgr] + rm, [Ar], bias=lb, scale=oml)
                self.P.add("dve", lambda hd: hd.tensor_tensor_scan(Bb[:, 0:Lg], self.scanmask[:, 0:Lg], A[:, 0:Lg],
                                                                   0.0, ALU.mult, ALU.add),
                           [Ar, self.r_const], [Br])
                B3 = Bb[:, 0:Lg].rearrange("p (a b) -> p a b", b=CH)
                S = lambda k, d=d: scal[:, d, k, 0:nch]
                self.cp(S(0).unsqueeze(2), B3[:, :, 31:32], [Br], [scalr])
                self.tt(B3, B3, S(0).unsqueeze(2).broadcast_to([128, nch, CH]), ALU.subtract, [Br, scalr], [Br])
                self.cp(S(1).unsqueeze(2), B3[:, :, 63:64], [Br], [scalr])
                if d == 1:
                    self.tt(Bb[:, 0:Lg], Bb[:, 0:Lg], A[:, 0:Lg], ALU.subtract, [Br, Ar], [Br])
                self.tt(S(2), S(0), S(1), ALU.add, [scalr], [scalr])
                self.act(S(2), S(2), AF.Exp, [scalr], [scalr])
                self.act(S(3), S(0), AF.Exp, [scalr], [scalr])
                self.act(S(4), S(1), AF.Exp, [scalr], [scalr])
                self.ts(S(0), S(3), -1.0, None, ALU.mult, None, [scalr], [scalr])
                self.ts(S(1), S(4), -1.0, None, ALU.mult, None, [scalr], [scalr])
                if d == 0:
                    self.act(A[:, 0:Lg], Bb[:, 0:Lg], AF.Exp, [Br], [Ar])
                    self.act(C[:, 0:Lg], Bb[:, 0:Lg], AF.Exp, [Br] + rm, [Cr], bias=lnoml, scale=-1.0)
                else:
                    self.act(A[:, 0:Lg], Bb[:, 0:Lg], AF.Exp, [Br], [Ar], scale=-1.0)
                    self.act(C[:, 0:Lg], Bb[:, 0:Lg], AF.Exp, [Br] + rm, [Cr], bias=lnoml)
                qt, qtr = Qt[d]
                kt_, ktr = Kt[d]
                self.tt(qt[:, 0:Lg], qs[:, 0:Lg], A[:, 0:Lg], ALU.mult, [qsr, Ar], [qtr])
                self.stt(kt_[:, 0:Lg], sg[:, 0:Lg], -1.0, C[:, 0:Lg], ALU.add, ALU.mult, [sgr, Cr], [ktr])

        def emit_chain_o(h, hh, fill):
            it, itr = itoks[h % 2]
            kSD = [(3, 1), (4, 0)]
            for si, (s0, L) in enumerate(seqs):
                tl = list(range(s0 // 128, (s0 + L) // 128))
                order = [tl, tl[::-1]]
                Scur = []
                for d in range(2):
                    Sc, Scr = Sb[d][0]
                    if sample:
                        srcd = (self.s0f if d == 0 else self.s0b)[h]
                        self.dma("sp", Sc, srcd, [], [Scr], self._ssem[d])
                    else:
                        self.P.add("dve", lambda hd, Sc=Sc: hd.memset(Sc, 0.0), [], [Scr])
                    Scur.append(0)
                for idx in range(len(tl)):
                    for d in range(2):
                        tt = order[d][idx]
                        qt, qtr = Qt[d]
                        kt_, ktr = Kt[d]
                        am, amr = attm[d]
                        bk2, bk2r = self.bank()
                        self.mm(bk2[:, 0:128], kt_[:, tt * 128:(tt + 1) * 128], qt[:, tt * 128:(tt + 1) * 128], True, True,
                                [ktr, qtr], [bk2r])
                        self.tt(am[:, tt, :], bk2[:, 0:128], self.nmask[d], ALU.mult, [bk2r, self.r_const], [amr])
                c_lo, c_n = s0 // CH, L // CH
                for d in range(2):
                    kt_, ktr = Kt[d]
                    kS, kD = kSD[d]
                    k3 = kt_[:, s0:s0 + L].rearrange("p (a b) -> p a b", b=CH)
                    self.tt(k3, k3, scal[:, d, kD, c_lo:c_lo + c_n].unsqueeze(2).broadcast_to([128, c_n, CH]),
                            ALU.mult, [ktr, scalr], [ktr])
                for idx in range(len(tl)):
                    dsb = {}
                    for d in range(2):
                        tt = order[d][idx]
                        kt_, ktr = Kt[d]
                        bk, bkr = self.bank()
                        bkb = bk[:, 0:64].bitcast(BF16)
                        self.tr(bkb, kt_[:, tt * 128:(tt + 1) * 128], self.ident_b, [ktr, self.r_const], [bkr])
                        ktk, ktkr = Ktok[d][tt % 2]
                        self.cp(ktk, bkb, [bkr], [ktkr])
                        for jj in ((0, 1) if d == 0 else (1, 0)):
                            c = 2 * tt + jj
                            bk3, bk3r = self.bank()
                            self.mm(bk3[:, 0:128], ktk[64 * jj:64 * jj + 64, :], it[64 * jj:64 * jj + 64, tt, :], True, True,
                                    [ktkr, itr], [bk3r])
                            dsb[(d, c)] = (bk3, bk3r)
                    for step in range(2):
                        for d in range(2):
                            tt = order[d][idx]
                            jj = ((0, 1) if d == 0 else (1, 0))[step]
                            c = 2 * tt + jj
                            kS, kD = kSD[d]
                            sp, spr = Sp[d]
                            Sc, Scr = Sb[d][Scur[d]]
                            Sn, Snr = Sb[d][1 - Scur[d]]
                            bk3, bk3r = dsb[(d, c)]
                            self.act(sp[:, c, :], Sc, AF.Copy, [Scr, scalr], [spr], scale=scal[:, d, kS, c:c + 1])
                            self.stt(Sn, Sc, scal[:, d, 2, c:c + 1], bk3[:, 0:128], ALU.mult, ALU.add, [Scr, scalr, bk3r], [Snr])
                            Scur[d] = 1 - Scur[d]
                    for _ in range(fill_per_idx):
                        if fill:
                            fill.pop(0)()
                if not sample:
                    for d in range(2):
                        Sc, Scr = Sb[d][Scur[d]]
                        dst = (self.nsf if d == 0 else self.nsb)[si, h]
                        self.dma("sp", dst, Sc, [Scr], [self.out_res[3 + d]], self._osem[d])
            for gi in range(0, ntile, 4):
                tts = list(range(gi, min(gi + 4, ntile)))
                for tt in tts:
                    c0 = (tt - gi) * 128
                    ops = []
                    for d in range(2):
                        ops.append((it[:, tt, :], attm[d][0][:, tt, :], c0, 128, [itr, attm[d][1]]))
                        for jj in range(2):
                            c = 2 * tt + jj
                            ops.append((Sp[d][0][:, c, :], Qt[d][0][:, c * CH:(c + 1) * CH], c0 + CH * jj, CH,
                                        [Sp[d][1], Qt[d][1]]))
                    for i, (l_, r_, cc, ww, rr) in enumerate(ops):
                        self.mm(ob[:, cc:cc + ww], l_, r_, i == 0, i == len(ops) - 1, rr, [obr])
                n = len(tts) * 128
                t0 = gi * 128
                self.act(oT[:, 0:n], ob[:, 0:n], AF.Copy, [obr, self.r_mod], [oTr], scale=self.gains[:, 2:3])
                sq, sqr = self.sqb[0], self.r_sqb[0]
                self.act(self.sqh[0][:, 0:n], ob[:, 0:n], AF.Square, [obr], [sqr])
                b2, b2r = self.bank()
                self.mm(b2[:, 0:n], self.ones_b, self.sqh[0][:, 0:n], True, True, [sqr, self.r_const], [b2r])
                self.rsqrt(b2, 128.0 * EPS, n, b2r)
                tf, tfr = self.tmpf[0], self.r_tmpf[0]
                self.tt(tf[:, 0:n], oT[:, 0:n], self.rstd[:, 0:n], ALU.mult, [oTr, self.r_rstd], [tfr])
                if sample:
                    self.tt(otmp[:, t0:t0 + n], tf[:, 0:n], gsT[:, hh, t0:t0 + n], ALU.mult, [tfr, gsTr], [otmpr])
                else:
                    self.tt(self.mixT[:, 8 + h, t0:t0 + n], tf[:, 0:n], gsT[:, hh, t0:t0 + n], ALU.mult, [tfr, gsTr],
                            [self.r_mix[8 + h]])
            if sample:
                self.extract(self.mixT[:, 8 + h, 0:SEG], [self.r_mix[8 + h]], otmp, [otmpr])

        n_idx = sum(L // 128 for (_, L) in seqs)
        fill_per_idx = 2 if sample else 2
        for half in range(2):
            emit_hg(half)
            for u in proj_head_units(half * 4):
                u()
            emit_prep(half * 4)
            for hh in range(4):
                h = half * 4 + hh
                fill = []
                if hh < 3:
                    fill += proj_head_units(h + 1)
                emit_chain_o(h, hh, fill)
                while fill:
                    fill.pop(0)()
                if hh < 3:
                    emit_prep(h + 1)
                if sample:
                    blks = [(1, h * 3 + k) for k in range(3)]
                else:
                    blks = [(0, 8 + h * 2 + k) for k in range(2)]
                sts = [self.adaln_mm(l_, nb_, k) for k, (l_, nb_) in enumerate(blks)]
                for k, (l_, nb_) in enumerate(blks):
                    self.adaln_tr(l_, nb_, sts[k])
        if sample:
            self.adaln_finish(1)
        else:
            self.adaln_finish(0, part=1)
        if "hg" in self.dbg:
            self.dump("d_hg%d" % v, self.mixT[:, 8:16, :].rearrange("p a b -> p (a b)"), self.r_mix[8:16], 8 * NP, BF16)

    MAIN_TILES = [(0, NP), (NP, SEG)]

    def phase_mlp(self, l):
        P = self.P
        P.barrier()
        self.norm_mod(self.xmain, [(0, NP, 0, 0), (NP, SEG, 1, NP)], l, 3)
        P.barrier()
        uT = [self.ar("uT%d" % i, i * 6272, 6272, [128, 16, NM], BF16) for i in range(2)]
        hrhs = lambda kc, t0, n: self.hT[:, kc, t0:t0 + n]
        for s in range(4):
            u, ur = uT[s % 2]
            for j in range(4):
                w, wr = self.wnext((WB_MLP0 if l == 0 else WB_MLP1) + s * 8 + j)

                def ev1(m, t0, n, bk, bkr, j=j):
                    tf, tfr = self.tmpf[m % 2], self.r_tmpf[m % 2]
                    self.act(tf[:, 0:n], bk[:, 0:n], AF.Relu, [bkr], [tfr])
                    self.tt(u[:, j * 4 + m, t0:t0 + n], tf[:, 0:n], tf[:, 0:n], ALU.mult, [tfr], [ur])
                self.proj(w, wr, range(4), self.MAIN_TILES, hrhs, [self.r_hT], ev1)
            urhs = lambda kc, t0, n, u=u: u[:, kc, t0:t0 + n]
            for cb in range(4):
                w, wr = self.wnext((WB_MLP0 if l == 0 else WB_MLP1) + s * 8 + 4 + cb)

                def ev2(m, t0, n, bk, bkr, cb=cb):
                    c = cb * 4 + m
                    v = 0 if t0 < NP else 1
                    xa, xr = self.xmain(c, t0, n)
                    self.stt(xa, bk[:, 0:n], self.modx[:, l, v, 5, c:c + 1], xa, ALU.mult, ALU.add,
                             [bkr, xr, self.r_mod], [xr])
                self.proj(w, wr, range(4), self.MAIN_TILES, urhs, [ur], ev2)
        if "x2" in self.dbg and l == 0:
            self.dump("d_x2p", self.xTp[:, :, :].rearrange("p a b -> p (a b)"), self.r_xTp, NCH * NP)
            self.dump("d_x2s", self.xTs_flat[:, :], self.r_xTs, NCH * SEG)

    def phase_pool(self):
        P = self.P
        P.barrier()
        self.norm_mod(self.xmain, [(0, NP, 0, 0), (NP, SEG, 1, NP)], 1, 0)
        P.barrier()
        PW = 512 + 3 * SEG
        ptab, ptabr = self.ar("ptab", 0, 2 * PW, [128, 4, PW], BF16)
        psem = P.dsem("ptab", "pool")
        self.dma("pool", ptab[:, :, :].rearrange("p a b -> p (a b)"), self.ptab_d, [], [ptabr], psem)
        z = [self.ar("z%d" % i, 2 * PW + i * 1792, 1792, [128, 7, 512], BF16) for i in range(2)]
        ttiles = [(t0, 128) for t0 in range(0, NP, 128)] + [(NP, 128), (NP + 128, 128), (NP + 256, 16)]
        w, wr = self.wnext(WB_POOL)

        def emit_z(g):
            zz, zr = z[g % 2]
            for ti, (t0, n) in enumerate(ttiles):
                bk, bkr = self.bank()
                for kc in range(4):
                    self.mm(bk[0:n, :], self.hT[:, g * 4 + kc, t0:t0 + n], w[:, g * 4 + kc, :], kc == 0, kc == 3,
                            [wr, self.r_hT], [bkr])
                self.act(zz[0:n, ti, :], bk[0:n, :], AF.Copy, [bkr], [zr])

        emit_z(0)
        for g in range(4):
            zz, zr = z[g % 2]
            if g + 1 < 4:
                emit_z(g + 1)
            for m in range(4):
                c = g * 4 + m
                for s in range(2):
                    bk, bkr = self.bank()
                    for ti in range(2):
                        self.mm(bk[:, 0:256], zz[:, s * 2 + ti, m * 128:(m + 1) * 128], ptab[:, g, ti * 256:(ti + 1) * 256],
                                ti == 0, ti == 1, [zr, ptabr], [bkr])
                    xa, xr = self.xTp[:, c, s * 256:(s + 1) * 256], self.r_xTp[c]
                    self.stt(xa, bk[:, 0:256], self.modx[:, 1, 0, 6, c:c + 1], xa, ALU.mult, ALU.add,
                             [bkr, xr, self.r_mod], [xr])
                bk, bkr = self.bank()
                for ti in range(3):
                    n = 128 if ti < 2 else 16
                    self.mm(bk[:, 0:SEG], zz[0:n, 4 + ti, m * 128:(m + 1) * 128],
                            ptab[0:n, g, 512 + ti * SEG:512 + (ti + 1) * SEG], ti == 0, ti == 2, [zr, ptabr], [bkr])
                xa, xr = self.xTs[:, c, :], self.r_xTs[c]
                self.stt(xa, bk[:, 0:SEG], self.modx[:, 1, 1, 6, c:c + 1], xa, ALU.mult, ALU.add,
                         [bkr, xr, self.r_mod], [xr])
        if "x3" in self.dbg:
            self.dump("d_x3p", self.xTp[:, :, :].rearrange("p a b -> p (a b)"), self.r_xTp, NCH * NP)
            self.dump("d_x3s", self.xTs_flat[:, :], self.r_xTs, NCH * SEG)

    def phase_final(self):
        P = self.P
        P.barrier()
        fint, fintr = self.ar("fint", 0, 2048, [128, 2048])
        fsem = P.dsem("fint", "sp")
        self.dma("sp", fint, self.finrep_d, [], [fintr], fsem)
        yst = [self.ar("yst%d" % i, 2048 + i * 2048, 2048, [128, 2048]) for i in range(2)]
        ssq, ssqr = self.ar("ssq", 6144, 16, [128, 16])
        junk, junkr = self.sqh[0], self.r_sqb[0]
        ysem = [P.dsem("yo0", "sp"), P.dsem("yo1", "sp")]
        ttiles = [(t0, 128) for t0 in range(0, NP, 128)] + [(NP, 128), (NP + 128, 128), (NP + 256, 16)]
        for ti, (t0, n) in enumerate(ttiles):
            ys, ysr = yst[ti % 2]
            sc = ssq[:, (ti % 2) * 8:(ti % 2) * 8 + 8]
            for cg in range(4):
                b2, b2r = self.bank()
                for j in range(4):
                    c = cg * 4 + j
                    xa, xr = self.xmain(c, t0, n)
                    self.tr(b2[0:n, j * 128:(j + 1) * 128], xa, self.ident_f, [xr, self.r_const], [b2r])
                self.P.add("act", lambda h, b2=b2, n=n, cg=cg, sc=sc: h.activation(
                    junk[0:n, 0:512], b2[0:n, :], AF.Square, accum_out=sc[0:n, cg:cg + 1]), [b2r], [junkr, ssqr])
                self.act(ys[0:n, cg * 512:(cg + 1) * 512], b2[0:n, :], AF.Copy, [b2r], [ysr])
            self.tt(sc[0:n, 4:5], sc[0:n, 0:1], sc[0:n, 1:2], ALU.add, [ssqr, ysr], [ssqr])
            self.tt(sc[0:n, 5:6], sc[0:n, 2:3], sc[0:n, 3:4], ALU.add, [ssqr], [ssqr])
            self.tt(sc[0:n, 6:7], sc[0:n, 4:5], sc[0:n, 5:6], ALU.add, [ssqr], [ssqr])
            self.act(sc[0:n, 7:8], sc[0:n, 6:7], AF.Ln, [ssqr, self.r_mod], [ssqr], bias=self.cconst(EPS)[0:n, :],
                     scale=1.0 / D)
            self.act(sc[0:n, 7:8], sc[0:n, 7:8], AF.Exp, [ssqr], [ssqr], scale=-0.5)
            self.stt(ys[0:n, :], ys[0:n, :], sc[0:n, 7:8], fint[0:n, :], ALU.mult, ALU.mult, [ysr, ssqr, fintr], [ysr])
            self.dma("sp", self.y[t0:t0 + n, :], ys[0:n, :], [ysr], [self.out_res[0]], ysem[ti % 2])

_CACHE = {}


def _run(inputs, stage=99, dbg=(), cores=tuple(range(8))):
    f = lambda k: np.asarray(inputs[k], np.float32)
    x_prompt, x_sample = f("x_prompt"), f("x_sample")
    cache_k, cache_v = f("cache_k"), f("cache_v")
    sf, sbw = f("state_hgrn_fwd"), f("state_hgrn_bwd")
    wb = _build_weight_blocks(f("w_ada"), f("w_in_ab"), f("w_out_ab"), f("w_pool"), f("w_mlp_in"),
                              f("w_mlp_out"))
    cf32 = _const_f32()
    cb16 = _const_b16()
    rope = _rope_tables().reshape(128, 2 * LS)
    finrep = np.ascontiguousarray(np.broadcast_to(f("final_norm")[None, :], (128, D)))
    in_maps = []
    for core in cores:
        b, q = core // 4, core % 4
        s0 = SEG_STARTS[q]
        xm = np.concatenate([x_prompt[2 * core].reshape(256, D), x_prompt[2 * core + 1].reshape(256, D),
                             x_sample[b, s0:s0 + SEG]], axis=0)
        in_maps.append({
            "xm": np.ascontiguousarray(xm),
            "xs": np.ascontiguousarray(x_sample[b]),
            "ck": np.ascontiguousarray(cache_k[b, 0].reshape(256, 256)),
            "cv": np.ascontiguousarray(cache_v[b, 0].reshape(256, 256)),
            "s0f": np.ascontiguousarray(sf[b, 0]),
            "s0b": np.ascontiguousarray(sbw[b, 0]),
            "vecs": _build_vecs(core, f("c"), f("c_ctx"), f("b_ada"), f("norm_mix"), f("norm_mlp"),
                                f("q_norm"), f("k_norm"), f("hg_norm"), f("lb_raw"), f("pool_scale"),
                                f("final_norm")),
            "cf32": cf32, "cb16": cb16, "rope": rope, "ptab": _ptab(q), "wb": wb,
            "finrep": finrep,
            "ropeseg": np.ascontiguousarray(rope.reshape(128, 2, LS)[:, :, s0:s0 + SEG]).reshape(128, 2 * SEG),
        })
    B0 = Builder(stage=stage, dbg=dbg)
    B0.build()
    B = Builder(stage=stage, dbg=dbg, wsched=list(B0.req))
    nc = B.build()
    res = run_bass_kernel_spmd(nc, in_maps, core_ids=list(range(len(cores))))
    return B, res


def kernel(**inputs):
    B, res = _run(inputs)
    R = res.results
    y_prompt = np.zeros((16, 256, D), np.float32)
    y_sample = np.zeros((2, LS, D), np.float32)
    new_k = np.zeros((16, 1, 256, 2, 128), np.float32)
    new_v = np.zeros((16, 1, 256, 2, 128), np.float32)
    nsf = np.zeros((16, 1, 8, 128, 128), np.float32)
    nsb = np.zeros((16, 1, 8, 128, 128), np.float32)
    for core in range(8):
        r = R[core]
        b, q = core // 4, core % 4
        y = r["y"]
        y_prompt[2 * core] = y[0:256]
        y_prompt[2 * core + 1] = y[256:512]
        off = q * 256 - SEG_STARTS[q]
        y_sample[b, q * 256:(q + 1) * 256] = y[512 + off:512 + off + 256]
        for s in range(2):
            new_k[2 * core + s, 0] = r["nk"][s * 256:(s + 1) * 256].reshape(256, 2, 128)
            new_v[2 * core + s, 0] = r["nv"][s * 256:(s + 1) * 256].reshape(256, 2, 128)
            nsf[2 * core + s, 0] = r["nsf"][s]
            nsb[2 * core + s, 0] = r["nsb"][s]
    return (y_prompt, y_sample, new_k, new_v, nsf, nsb)
```
